# Optimizing a Trainium2 kernel written in Bass

```python
import math
import jax, jax.numpy as jnp
from jax import lax
import numpy as np

D_MODEL = 1024
BATCH = 2
SEQ = 16384
DEPTH = 1
DEC_BATCH = 32
DEC_SEQ = 64
PAST_LEN = 2048

CHUNK = 64
N_PREV_CHUNKS = 8
BAND_PAST = N_PREV_CHUNKS * CHUNK
BAND_LEN = BAND_PAST + CHUNK
N_HEADS = 8
HEAD_DIM = 64
ATTN_DIM = N_HEADS * HEAD_DIM
CONV_DIM = D_MODEL // 2
CONV_K = 31
MAX_REL = 128
FFN_DIM = ((8 * D_MODEL + 3 * 256 - 1) // (3 * 256)) * 256
N_IN = 2 * CONV_DIM + 3 * ATTN_DIM + 2 * D_MODEL
ATTN_SCALE = 1.0 / math.sqrt(HEAD_DIM)
NORM_EPS = 1e-6
NEG_INF = -1e30

kernel_name = "streaming_conformer_conv_band_attn_hybrid"


def rms_norm(x, g):
    xf = x.astype(jnp.float32)
    y = xf * lax.rsqrt(jnp.mean(xf * xf, axis=-1, keepdims=True) + NORM_EPS)
    return (y * g.astype(jnp.float32)).astype(x.dtype)


def layer_norm(x, g, b):
    xf = x.astype(jnp.float32)
    mu = jnp.mean(xf, axis=-1, keepdims=True)
    var = jnp.mean(jnp.square(xf - mu), axis=-1, keepdims=True)
    y = (xf - mu) * lax.rsqrt(var + NORM_EPS)
    return (y * g.astype(jnp.float32) + b.astype(jnp.float32)).astype(x.dtype)


def conv_module(u, hist, w_dw, b_dw, ln_g, ln_b, w_out):
    ext = jnp.concatenate([hist, u], axis=1)
    y = lax.conv_general_dilated(
        ext, w_dw[:, None, :], window_strides=(1,), padding='VALID',
        dimension_numbers=('NWC', 'WIO', 'NWC'), feature_group_count=CONV_DIM)
    y = jax.nn.silu(layer_norm(y + b_dw, ln_g, ln_b))
    return y @ w_out, ext[:, -(CONV_K - 1):]


def band_attend(q, k, v, q_pos, k_pos, rel_bias):
    s = jnp.einsum('bqhd,bkhd->bhqk', q, k).astype(jnp.float32) * ATTN_SCALE
    rel = jnp.clip(q_pos[:, None] - k_pos[None, :], -MAX_REL, MAX_REL) + MAX_REL
    s = s + rel_bias[:, rel].astype(jnp.float32)[None]
    qc = (q_pos // CHUNK)[:, None]
    kc = (k_pos // CHUNK)[None, :]
    mask = (k_pos[None, :] >= 0) & (kc <= qc) & (kc >= qc - N_PREV_CHUNKS)
    s = jnp.where(mask[None, None], s, NEG_INF)
    p = jax.nn.softmax(s, axis=-1).astype(v.dtype)
    return jnp.einsum('bhqk,bkhd->bqhd', p, v)


def prompt_band_attention(q, k, v, rel_bias):
    B, T, H, Dh = q.shape
    nc = T // CHUNK
    pad = ((0, 0), (BAND_PAST, 0), (0, 0), (0, 0))
    kp = jnp.pad(k, pad)
    vp = jnp.pad(v, pad)
    qc = q.reshape(B, nc, CHUNK, H, Dh).swapaxes(0, 1)

    def one_chunk(args):
        qn, n = args
        start = n * CHUNK
        kb = lax.dynamic_slice_in_dim(kp, start, BAND_LEN, axis=1)
        vb = lax.dynamic_slice_in_dim(vp, start, BAND_LEN, axis=1)
        q_pos = start + jnp.arange(CHUNK, dtype=jnp.int32)
        k_pos = start - BAND_PAST + jnp.arange(BAND_LEN, dtype=jnp.int32)
        return band_attend(qn, kb, vb, q_pos, k_pos, rel_bias)

    o = lax.map(one_chunk, (qc, jnp.arange(nc, dtype=jnp.int32)))
    return o.swapaxes(0, 1).reshape(B, T, H, Dh)


def encoder_layer(x, c, conv_hist, attend, p):
    B, T, _ = x.shape
    mod = jax.nn.silu(c) @ p['w_ada'] + p['b_ada']
    sh1, sc1, gt1, sh2, sc2, gt2 = [m[:, None, :] for m in jnp.split(mod, 6, axis=-1)]

    h = rms_norm(x, p['norm1_g']) * (1 + sc1) + sh1
    z = h @ p['w_in']
    offs = np.cumsum([CONV_DIM, CONV_DIM, ATTN_DIM, ATTN_DIM, ATTN_DIM, D_MODEL])
    glu_a, glu_b, q, k, v, g_conv, g_attn = jnp.split(z, offs, axis=-1)

    u = glu_a * jax.nn.sigmoid(glu_b)
    conv_out, conv_state = conv_module(u, conv_hist, p['w_dw'], p['b_dw'],
                                       p['conv_ln_g'], p['conv_ln_b'], p['w_conv_out'])

    q = rms_norm(q.reshape(B, T, N_HEADS, HEAD_DIM), p['q_norm_g'])
    k = rms_norm(k.reshape(B, T, N_HEADS, HEAD_DIM), p['k_norm_g'])
    v = v.reshape(B, T, N_HEADS, HEAD_DIM)
    o = attend(q, k, v, p['rel_bias']).reshape(B, T, ATTN_DIM)
    attn_out = o @ p['w_attn_out']

    merged = jax.nn.sigmoid(g_conv) * conv_out + jax.nn.sigmoid(g_attn) * attn_out
    x = x + gt1 * (merged @ p['w_o'])

    h2 = rms_norm(x, p['norm2_g']) * (1 + sc2) + sh2
    gate, up = jnp.split(h2 @ p['w_ffn_in'], 2, axis=-1)
    x = x + gt2 * ((jax.nn.silu(gate) * up) @ p['w_ffn_out'])
    return x, conv_state, k, v


def setup_inputs(seed: int = 0) -> dict:
    key = jax.random.key(seed)
    ks = iter(jax.random.split(key, 32))
    f32 = jnp.float32
    cache_len = min(BAND_PAST, PAST_LEN)

    def nrm(shape, scale):
        return jax.random.normal(next(ks), shape, f32) * scale

    return {
        "x_prompt": nrm((BATCH, SEQ, D_MODEL), 1.0),
        "x_sample": nrm((DEC_BATCH, DEC_SEQ, D_MODEL), 1.0),
        "c_prompt": nrm((BATCH, D_MODEL), 1.0),
        "c_sample": nrm((DEC_BATCH, D_MODEL), 1.0),
        "cache_conv": nrm((DEPTH, DEC_BATCH, CONV_K - 1, CONV_DIM), 1.0),
        "cache_k": nrm((DEPTH, DEC_BATCH, cache_len, N_HEADS, HEAD_DIM), 1.0),
        "cache_v": nrm((DEPTH, DEC_BATCH, cache_len, N_HEADS, HEAD_DIM), 1.0),
        "norm1_g": 1.0 + nrm((DEPTH, D_MODEL), 0.02),
        "norm2_g": 1.0 + nrm((DEPTH, D_MODEL), 0.02),
        "w_ada": nrm((DEPTH, D_MODEL, 6 * D_MODEL), 0.5 * D_MODEL ** -0.5),
        "b_ada": nrm((DEPTH, 6 * D_MODEL), 0.02),
        "w_in": nrm((DEPTH, D_MODEL, N_IN), D_MODEL ** -0.5),
        "w_dw": nrm((DEPTH, CONV_K, CONV_DIM), CONV_K ** -0.5),
        "b_dw": nrm((DEPTH, CONV_DIM), 0.02),
        "conv_ln_g": 1.0 + nrm((DEPTH, CONV_DIM), 0.02),
        "conv_ln_b": nrm((DEPTH, CONV_DIM), 0.02),
        "w_conv_out": nrm((DEPTH, CONV_DIM, D_MODEL), CONV_DIM ** -0.5),
        "q_norm_g": 1.0 + nrm((DEPTH, HEAD_DIM), 0.02),
        "k_norm_g": 1.0 + nrm((DEPTH, HEAD_DIM), 0.02),
        "rel_bias": nrm((DEPTH, N_HEADS, 2 * MAX_REL + 1), 0.5),
        "w_attn_out": nrm((DEPTH, ATTN_DIM, D_MODEL), ATTN_DIM ** -0.5),
        "w_o": nrm((DEPTH, D_MODEL, D_MODEL), D_MODEL ** -0.5),
        "w_ffn_in": nrm((DEPTH, D_MODEL, 2 * FFN_DIM), D_MODEL ** -0.5),
        "w_ffn_out": nrm((DEPTH, FFN_DIM, D_MODEL), FFN_DIM ** -0.5),
    }


def reference(x_prompt, x_sample, c_prompt, c_sample, cache_conv, cache_k, cache_v,
              norm1_g, norm2_g, w_ada, b_ada, w_in, w_dw, b_dw, conv_ln_g, conv_ln_b,
              w_conv_out, q_norm_g, k_norm_g, rel_bias, w_attn_out, w_o, w_ffn_in, w_ffn_out):
    t_new = x_sample.shape[1]
    prompt_state_len = min(BAND_PAST, x_prompt.shape[1])
    xp, xs = x_prompt, x_sample
    conv_p_l, kp_l, vp_l, conv_s_l, ks_l, vs_l = [], [], [], [], [], []

    for l in range(DEPTH):
        p = {
            'norm1_g': norm1_g[l], 'norm2_g': norm2_g[l], 'w_ada': w_ada[l], 'b_ada': b_ada[l],
            'w_in': w_in[l], 'w_dw': w_dw[l], 'b_dw': b_dw[l], 'conv_ln_g': conv_ln_g[l],
            'conv_ln_b': conv_ln_b[l], 'w_conv_out': w_conv_out[l], 'q_norm_g': q_norm_g[l],
            'k_norm_g': k_norm_g[l], 'rel_bias': rel_bias[l], 'w_attn_out': w_attn_out[l],
            'w_o': w_o[l], 'w_ffn_in': w_ffn_in[l], 'w_ffn_out': w_ffn_out[l],
        }

        hist0 = jnp.zeros((xp.shape[0], CONV_K - 1, CONV_DIM), xp.dtype)
        xp, conv_p, k_p, v_p = encoder_layer(xp, c_prompt, hist0, prompt_band_attention, p)
        conv_p_l.append(conv_p)
        kp_l.append(k_p[:, -prompt_state_len:])
        vp_l.append(v_p[:, -prompt_state_len:])

        ck, cv = cache_k[l], cache_v[l]
        cache_len = ck.shape[1]

        def sample_attend(q, k, v, rb, ck=ck, cv=cv, cache_len=cache_len):
            k_all = jnp.concatenate([ck, k], axis=1)
            v_all = jnp.concatenate([cv, v], axis=1)
            q_pos = PAST_LEN + jnp.arange(t_new, dtype=jnp.int32)
            k_pos = jnp.concatenate(
                [PAST_LEN - cache_len + jnp.arange(cache_len, dtype=jnp.int32), q_pos])
            return band_attend(q, k_all, v_all, q_pos, k_pos, rb)

        xs, conv_s, k_s, v_s = encoder_layer(xs, c_sample, cache_conv[l], sample_attend, p)
        conv_s_l.append(conv_s)
        ks_l.append(k_s)
        vs_l.append(v_s)

    conv_state_prompt = jnp.stack(conv_p_l)
    k_state_prompt = jnp.stack(kp_l)
    v_state_prompt = jnp.stack(vp_l)
    conv_state_sample = jnp.stack(conv_s_l)
    k_new_sample = jnp.stack(ks_l)
    v_new_sample = jnp.stack(vs_l)
    return (xp, xs, conv_state_prompt, k_state_prompt, v_state_prompt,
            conv_state_sample, k_new_sample, v_new_sample)
```

```python
import os
import numpy as np
import concourse.bass as bass
import concourse.mybir as mybir
from concourse.bass_utils import run_bass_kernel_spmd
from contextlib import ExitStack

F32 = mybir.dt.float32
BF16 = mybir.dt.bfloat16
AF = mybir.ActivationFunctionType
ALU = mybir.AluOpType
AX = mybir.AxisListType

D = 1024
NPB = 32
NHB = 4
EPS = 1e-6
FFN = 2816
ARENA_WORDS = 53184
CONV_TAPS_PER_SEG = 1
CONV_BOOST = True
CHAIN_ORDER = ("S2", "A", "S3")


ENGS = ("pe", "act", "dve", "pool", "sp")
SEM_EPOCH = 30000
FUSE_WAITS = True


class Res:
    __slots__ = ("name", "last_w", "readers", "dsem", "dcount", "aliases", "excl")

    def __init__(self, name):
        self.name = name
        self.last_w = None
        self.readers = []
        self.dsem = None
        self.dcount = 0
        self.aliases = []
        self.excl = False


class Op:
    __slots__ = ("eng", "fn", "deps", "needs_sig", "sig", "is_dma", "dres",
                 "dtarget", "gidx", "dma_waits", "eidx")

    def __init__(self, eng, fn, is_dma, gidx):
        self.eng = eng
        self.fn = fn
        self.deps = []
        self.needs_sig = False
        self.sig = None
        self.is_dma = is_dma
        self.dres = None
        self.dtarget = 0
        self.gidx = gidx
        self.dma_waits = []


class Sched:
    def __init__(self, nc):
        self.nc = nc
        self.ops = {e: [] for e in ENGS}
        self.n = 0
        self.dma_res = []

    def res(self, name):
        return Res(name)

    def op(self, eng, fn, reads=(), writes=(), dma=False, dres=None):
        o = Op(eng, fn, dma, self.n)
        o.eidx = len(self.ops[eng])
        self.n += 1
        deps = {}

        def add(d, kind):
            if d is o:
                return
            k = id(d)
            if k in deps:
                if kind == "raw":
                    deps[k] = (d, "raw")
            else:
                deps[k] = (d, kind)

        for r in reads:
            if r.last_w is not None:
                add(r.last_w, "raw")
            if r.excl:
                for rd in r.readers:
                    if rd.eng != eng:
                        add(rd, "war")
        for w in writes:
            if w.last_w is not None:
                add(w.last_w, "waw")
            for rd in w.readers:
                add(rd, "war")
            if w.aliases:
                for a in w.aliases:
                    if a.last_w is not None:
                        add(a.last_w, "waw")
                    for rd in a.readers:
                        add(rd, "war")
                w.aliases = []
        for d, kind in deps.values():
            if d.is_dma:
                o.dma_waits.append((d.dres, d.dres.dcount))
            else:
                if d.eng == o.eng and not o.is_dma:
                    if o.eng == "pe":
                        continue
                d.needs_sig = True
                o.deps.append(d)
        for r in reads:
            if not o.is_dma:
                r.readers = [x for x in r.readers if x.is_dma or x.eng != o.eng]
            r.readers.append(o)
        for w in writes:
            w.last_w = o
            w.readers = []
        if dma:
            assert dres is not None
            if dres.dsem is None:
                self.dma_res.append(dres)
                dres.dsem = True
            dres.dcount += 16
            o.dres = dres
            o.dtarget = dres.dcount
        self.ops[eng].append(o)
        return o

    def emit(self, extra_ctx=()):
        nc = self.nc
        from contextlib import ExitStack
        nsem = {}
        for e in ENGS:
            c = 0
            for o in self.ops[e]:
                if o.needs_sig:
                    o.sig = (c // SEM_EPOCH, c % SEM_EPOCH + 1)
                    c += 1
            nsem[e] = max(1, (c + SEM_EPOCH - 1) // SEM_EPOCH)
        with ExitStack() as st:
            esem = {e: [st.enter_context(nc.semaphore(f"s_{e}{i}")) for i in range(nsem[e])]
                    for e in ENGS}
            for r in self.dma_res:
                r.dsem = st.enter_context(nc.semaphore(f"d_{r.name}"))
            block = st.enter_context(nc.Block())
            handles = {"pe": block.tensor, "act": block.scalar, "dve": block.vector,
                       "pool": block.gpsimd, "sp": block.sync}
            final_dma = [(r.dsem, r.dcount) for r in self.dma_res]

            def make(e):
                ops = self.ops[e]

                def body(eng):
                    waited = {}
                    for o in ops:
                        ws = []
                        for d in o.deps:
                            ep, v = d.sig
                            ws.append((esem[d.eng][ep], v))
                        for (r, v) in o.dma_waits:
                            ws.append((r.dsem, v))
                        need = []
                        for (s, v) in ws:
                            k = id(s)
                            if waited.get(k, 0) >= v:
                                continue
                            waited[k] = v
                            need.append((s, v))
                        fuse = None
                        if need and FUSE_WAITS and not o.is_dma and not getattr(o.fn, "multi", False):
                            fuse = need.pop()
                        for (s, v) in need:
                            eng.wait_ge(s, v)
                        ins = o.fn(eng)
                        if fuse is not None:
                            ins._wait_ge(fuse[0], fuse[1])
                        if o.is_dma:
                            ins.then_inc(o.dres.dsem, 16)
                        elif o.needs_sig:
                            ins.then_inc(esem[e][o.sig[0]], 1)
                    if e == "sp":
                        for (s, v) in final_dma:
                            if waited.get(id(s), 0) < v:
                                eng.wait_ge(s, v)
                return body

            for e in ENGS:
                if self.ops[e] or e == "sp":
                    handles[e](make(e))


class Arena:
    def __init__(self, tensor, nwords):
        self.t = tensor
        self.n = nwords
        self.ptr = 0
        self.all = []

    def alloc(self, name, nelem, dt=F32):
        words = nelem if dt == F32 else (nelem + 1) // 2
        start, end = self.ptr, self.ptr + words
        assert end <= self.n, f"arena overflow at {name}: {end} > {self.n}"
        self.ptr = end
        res = Res(name)
        res.aliases = [r for (s, e, r) in self.all if s < end and e > start]
        self.all.append((start, end, res))
        ap = self.t[:, start:end]
        if dt == BF16:
            ap = ap.bitcast(BF16)[:, :nelem]
        return ap, res


def multi(fn):
    fn.multi = True
    return fn


def build():
    nc = bass.Bass("TRN2", target_bir_lowering=False)

    def din(name, shape):
        return nc.dram_tensor(name, list(shape), F32, kind="ExternalInput").ap()

    def dout(name, shape):
        return nc.dram_tensor(name, list(shape), F32, kind="ExternalOutput").ap()

    xp = din("xp", [NHB * 128 + NPB * 128, D])
    xs = din("xs", [256, D])
    cT = din("cT", [D, 5])
    flag_d = din("flag", [128, 1])
    cconv = din("cconv", [4, 30, 512])
    ck = din("ck", [4, 512, 512])
    cv = din("cv", [4, 512, 512])
    relbT = din("relbT", [4, 128, 8 * 128])
    w_ada = din("w_ada", [D, 6 * D])
    b_ada = din("b_ada", [1, 6 * D])
    gfm = din("gfm", [128, 16])
    w_in = din("w_in", [D, 4608])
    convp = din("convp", [128, 4 * 34])
    w_conv_out = din("w_conv_out", [512, D])
    gqk = din("gqk", [128, 2])
    gkrow = din("gkrow", [1, 512])
    w_attn_out = din("w_attn_out", [512, D])
    w_o = din("w_o", [D, D])
    w_ffn_in = din("w_ffn_in", [D, 2 * FFN])
    w_ffn_out = din("w_ffn_out", [FFN, D])

    y_p = dout("y_p", [NPB * 128, D])
    y_s = dout("y_s", [256, D])
    conv_p = dout("conv_p", [30, 512])
    k_p = dout("k_p", [512, 512])
    v_p = dout("v_p", [512, 512])
    conv_s = dout("conv_s", [4, 30, 512])
    k_s = dout("k_s", [256, 512])
    v_s = dout("v_s", [256, 512])

    modrows = nc.dram_tensor("modrows", [5, 6 * D], F32).ap()
    x1s = nc.dram_tensor("x1s", [(NPB + 2) * 128, D], F32).ap()

    S = Sched(nc)
    with ExitStack() as st:
        arena_t = st.enter_context(nc.sbuf_tensor("arena", [128, ARENA_WORDS], F32))
        ps_t = st.enter_context(nc.psum_tensor("ps", [128, 4096], F32))
        A = Arena(arena_t, ARENA_WORDS)
        banks = [(ps_t[:, 512 * i:512 * (i + 1)], Res(f"bank{i}")) for i in range(8)]
        for _, _r in banks:
            _r.excl = True
        R_modrows = Res("modrows")
        R_x1s = [Res(f"x1s{i}") for i in range(NPB + 2)]
        R_out = Res("outs")

        def op(eng, fn, r=(), w=()):
            return S.op(eng, fn, reads=r, writes=w)

        def dma(fn, r=(), w=(), dres=None, q="sp"):
            return S.op(q, fn, reads=r, writes=w, dma=True, dres=dres)

        identf, R_identf = A.alloc("identf", 128)
        ident, R_ident = A.alloc("ident", 128, BF16)
        onesc, R_onesc = A.alloc("onesc", 8)
        onesrow, R_onesrow = A.alloc("onesrow", 128)
        op("pool", lambda e: e.memset(identf, 1.0), w=[R_identf])
        op("pool", lambda e: e.affine_select(out=identf, in_=identf, pattern=[[-1, 128]],
                                             compare_op=ALU.is_equal, fill=0.0, base=0,
                                             channel_multiplier=1), r=[R_identf], w=[R_identf])
        op("dve", lambda e: e.tensor_copy(out=ident, in_=identf), r=[R_identf], w=[R_ident])
        op("pool", lambda e: e.memset(onesc[:, 0:1], 1.0), w=[R_onesc])
        op("pool", lambda e: e.memset(onesc[:, 1:2], -0.5), w=[R_onesc])
        op("pool", lambda e: e.memset(onesrow, 1.0), w=[R_onesrow])
        dma(lambda e: e.dma_start(out=onesc[:, 2:3], in_=flag_d), w=[R_onesc], dres=R_onesc)
        ONE = onesc[:, 0:1]
        MHALF = onesc[:, 1:2]
        FLAG = onesc[:, 2:3]

        gfm_t, R_gfm = A.alloc("gfm", 16)
        convp_t, R_convp = A.alloc("convp", 4 * 34)
        gqk_t, R_gqk = A.alloc("gqk", 2)
        gqs_t, R_gqs = A.alloc("gqs", 2)
        gkbc, R_gkbc = A.alloc("gkbc", 512)
        dma(lambda e: e.dma_start(out=gfm_t, in_=gfm), w=[R_gfm], dres=R_gfm)
        dma(lambda e: e.dma_start(out=convp_t, in_=convp), w=[R_convp], dres=R_convp)
        dma(lambda e: e.dma_start(out=gqk_t, in_=gqk), w=[R_gqk], dres=R_gqk)
        dma(lambda e: e.dma_start(out=gkbc, in_=gkrow.partition_broadcast(128)), w=[R_gkbc], dres=R_gkbc)
        for col in range(2):
            op("dve", lambda e, col=col: e.tensor_scalar(out=gqs_t[:, col:col + 1], in0=gqk_t[:, 0:1], scalar1=0.125,
                                                         scalar2=None, op0=ALU.mult), r=[R_gqk], w=[R_gqs])
        op("pool", lambda e: e.memset(gqs_t[64:128, 0:1], 0.0), r=[R_gqs], w=[R_gqs])
        op("pool", lambda e: e.memset(gqs_t[0:64, 1:2], 0.0), r=[R_gqs], w=[R_gqs])
        convp_v = convp_t.rearrange("p (c j) -> p c j", j=34)

        modT, R_modT = A.alloc("modT", 32 * 5)
        modT_v = modT.rearrange("p (i m) -> p i m", m=5)
        siluT, R_siluT = A.alloc("siluT", 40)
        siluT_v = siluT.rearrange("p (c m) -> p c m", m=5)
        gt1s = [A.alloc(f"gt1s{i}", D) for i in range(2)]
        expB, R_expB = A.alloc("expB", 4 * 1024, BF16)
        expB_v = expB.rearrange("p (k h q) -> p k h q", k=4, h=8)
        expBs, R_expBs = A.alloc("expBs", 512, BF16)
        expBs_v = expBs.rearrange("p (h q) -> p h q", h=8)
        const_end = A.ptr

        Win, R_Win = A.alloc("Win", 8 * 4608, BF16)
        Win_v = Win.rearrange("p (k n) -> p k n", k=8)
        Wco, R_Wco = A.alloc("Wco", 4 * 1024, BF16)
        Wco_v = Wco.rearrange("p (k n) -> p k n", k=4)
        Wao, R_Wao = A.alloc("Wao", 4 * 1024, BF16)
        Wao_v = Wao.rearrange("p (k n) -> p k n", k=4)
        Wo, R_Wo = A.alloc("Wo", 8 * 1024, BF16)
        Wo_v = Wo.rearrange("p (k n) -> p k n", k=8)
        w1_end = A.ptr

        cTt, R_cT = A.alloc("cTt", 40)
        dma(lambda e: e.dma_start(out=cTt.rearrange("p (c m) -> p c m", m=5),
                                  in_=cT.rearrange("(c p) m -> p c m", p=128)), w=[R_cT], dres=R_cT)
        op("act", lambda e: e.activation(out=siluT, in_=cTt, func=AF.Silu), r=[R_cT], w=[R_siluT])
        adast = [A.alloc(f"adast{i}", 8 * 512) for i in range(2)]
        bst = [A.alloc(f"bst{i}", 512) for i in range(2)]
        modblk = [A.alloc(f"modblk{i}", 512) for i in range(2)]
        wst = [A.alloc(f"wst{i}", 2304) for i in range(4)]
        relst, R_relst = A.alloc("relst", 1024)

        cast_rr = [0]
        cast_queues = [("pool",)]

        def load_cast_gen(dram_w, K, N, dst_v, R_dst, piece=2304):
            kch = K // 128
            for k in range(kch):
                for n0 in range(0, N, piece):
                    n1 = min(N, n0 + piece)
                    i = cast_rr[0]
                    cast_rr[0] += 1
                    stg, R_stg = wst[i % len(wst)]
                    qs = cast_queues[0]
                    dma(lambda e, stg=stg, k=k, n0=n0, n1=n1: e.dma_start(
                        out=stg[:, 0:n1 - n0], in_=dram_w[k * 128:(k + 1) * 128, n0:n1]),
                        w=[R_stg], dres=R_stg, q=qs[(i % len(wst)) % len(qs)])
                    if i % 2 == 1:
                        op("act", lambda e, stg=stg, k=k, n0=n0, n1=n1: e.activation(
                            out=dst_v[:, k, n0:n1], in_=stg[:, 0:n1 - n0], func=AF.Copy),
                           r=[R_stg], w=[R_dst])
                    else:
                        op("dve", lambda e, stg=stg, k=k, n0=n0, n1=n1: e.tensor_copy(
                            out=dst_v[:, k, n0:n1], in_=stg[:, 0:n1 - n0]), r=[R_stg], w=[R_dst])
                    yield

        def load_cast(*a_, **kw_):
            for _ in load_cast_gen(*a_, **kw_):
                pass

        def chain_gens(*gs):
            for g in gs:
                yield from g

        DBGS = int(os.environ.get("KDBG_SETUP", "9"))
        wgen = chain_gens(load_cast_gen(w_in, D, 4608, Win_v, R_Win),
                          load_cast_gen(w_conv_out, 512, D, Wco_v, R_Wco),
                          load_cast_gen(w_attn_out, 512, D, Wao_v, R_Wao),
                          load_cast_gen(w_o, D, D, Wo_v, R_Wo)) if DBGS >= 4 else iter(())

        kinds = {0: 0, 1: 0, 2: 1, 3: 1, 6: 2, 7: 2, 8: 3, 9: 3}
        for cb in range(12 if DBGS >= 2 else 0):
            sl = cb % 2
            ast, R_ast = adast[sl]
            bt, R_bt = bst[sl]
            mb, R_mb = modblk[sl]
            ast_v = ast.rearrange("p (k n) -> p k n", k=8)
            dma(lambda e, ast_v=ast_v, cb=cb: e.dma_start(
                out=ast_v, in_=w_ada[:, cb * 512:(cb + 1) * 512].rearrange("(k p) n -> p k n", p=128)),
                w=[R_ast], dres=R_ast)
            dma(lambda e, bt=bt, cb=cb: e.dma_start(
                out=bt[0:5, :], in_=b_ada[:, cb * 512:(cb + 1) * 512].partition_broadcast(5)),
                w=[R_bt], dres=R_bt)
            pb, R_pb = banks[cb % 2]
            for k in range(8):
                op("pe", lambda e, pb=pb, k=k, ast_v=ast_v: e.matmul(
                    pb[0:5, :], lhsT=siluT_v[:, k, :], rhs=ast_v[:, k, :], start=(k == 0), stop=(k == 7)),
                   r=[R_siluT, R_ast], w=[R_pb])
            op("dve", lambda e, pb=pb, mb=mb, bt=bt: e.tensor_tensor(
                out=mb[0:5, :], in0=pb[0:5, :], in1=bt[0:5, :], op=ALU.add), r=[R_pb, R_bt], w=[R_mb])
            dma(lambda e, mb=mb, cb=cb: e.dma_start(out=modrows[:, cb * 512:(cb + 1) * 512], in_=mb[0:5, :]),
                r=[R_mb], w=[R_modrows], dres=R_mb)
            for _ in range(3):
                next(wgen, None)
            if cb in kinds:
                kind = kinds[cb]
                pt, R_pt = banks[2 + cb % 2]
                for j in range(4):
                    op("pe", lambda e, pt=pt, mb=mb, j=j: e.transpose(
                        out=pt[:, j * 5:(j + 1) * 5], in_=mb[0:5, j * 128:(j + 1) * 128],
                        identity=identf[0:5, 0:5]), r=[R_mb, R_identf], w=[R_pt])
                base = kind * 8 + (cb % 2) * 4
                op("act", lambda e, pt=pt, base=base: e.activation(
                    out=modT[:, base * 5:(base + 4) * 5], in_=pt[:, 0:20], func=AF.Copy),
                   r=[R_pt], w=[R_modT])
        for kind, gcol in ((1, 0), (3, 8)):
            v = modT_v[:, kind * 8:(kind + 1) * 8, :]
            op("dve", lambda e, v=v: e.tensor_scalar(out=v, in0=v, scalar1=1.0, scalar2=None, op0=ALU.add),
               r=[R_modT], w=[R_modT])
            op("dve", lambda e, v=v, gcol=gcol: e.tensor_tensor(
                out=v, in0=v, in1=gfm_t[:, gcol:gcol + 8].unsqueeze(2).to_broadcast([128, 8, 5]), op=ALU.mult),
               r=[R_modT, R_gfm], w=[R_modT])

        def SH1(c, m): return modT_v[:, 0 + c, m:m + 1]
        def A1(c, m): return modT_v[:, 8 + c, m:m + 1]
        def SH2(c, m): return modT_v[:, 16 + c, m:m + 1]
        def A2(c, m): return modT_v[:, 24 + c, m:m + 1]

        def load_gt(off, tiles):
            for ap_, res_, parts in tiles:
                for (p0, p1, m) in parts:
                    dma(lambda e, ap_=ap_, p0=p0, p1=p1, m=m: e.dma_start(
                        out=ap_[p0:p1, :], in_=modrows[m:m + 1, off:off + D].partition_broadcast(p1 - p0)),
                        r=[R_modrows], w=[res_], dres=res_)

        if DBGS >= 3:
          load_gt(2 * D, [(gt1s[0][0], gt1s[0][1], [(0, 64, 1), (64, 128, 2)]),
                        (gt1s[1][0], gt1s[1][1], [(0, 64, 3), (64, 128, 4)])])

        for _ in wgen:
            pass
        if DBGS >= 4:
            gtmp, R_gtmp = relst, R_relst
            dma(lambda e: e.dma_start(out=gtmp, in_=modrows[0:1, 2 * D:3 * D].partition_broadcast(128)),
                r=[R_modrows], w=[R_gtmp], dres=R_gtmp)
            for k in range(8):
                op("dve", lambda e, k=k: e.tensor_tensor(out=Wo_v[:, k, :], in0=Wo_v[:, k, :], in1=gtmp, op=ALU.mult),
                   r=[R_Wo, R_gtmp], w=[R_Wo])

        for ti in range(4 if DBGS >= 5 else 0):
            dma(lambda e, ti=ti: e.dma_start(out=relst, in_=relbT[ti]), w=[R_relst], dres=R_relst)
            op("act", lambda e, ti=ti: e.activation(out=expB[:, ti * 1024:(ti + 1) * 1024], in_=relst,
                                                    func=AF.Copy), r=[R_relst], w=[R_expB])
        NEG = -30000.0
        op("pool", lambda e: e.memset(expB_v[0:64, 0, :, 64:128], NEG), r=[R_expB], w=[R_expB])
        op("pool", lambda e: e.memset(expB_v[64:128, 3, :, 0:64], NEG), r=[R_expB], w=[R_expB])
        op("pool", lambda e: e.tensor_copy(out=expBs_v, in_=expB_v[:, 3, :, 64:128]), r=[R_expB], w=[R_expBs])
        op("pool", lambda e: e.memset(expBs_v[0:64, :, :], NEG), r=[R_expBs], w=[R_expBs])
        TBL = [0, 1, 1, 2, 3]

        A.ptr = w1_end

        def alloc2(name, n, dt=F32):
            return [A.alloc(f"{name}{i}", n, dt) for i in range(2)]

        xA, R_xA = A.alloc("xA", D)
        xB, R_xB = A.alloc("xB", D)
        xn, R_xn = A.alloc("xn", D, BF16)
        hT2 = alloc2("hT", D, BF16)
        stat, R_stat = A.alloc("stat", 40)
        statn8, R_statn = A.alloc("statn", 8)
        statl, R_statl = A.alloc("statl", 8)
        sgl, R_sgl = A.alloc("sgl", 512)
        ufp, R_ufp = A.alloc("ufp", 512)
        ubf, R_ubf = A.alloc("ubf", 512, BF16)
        uT2raw = alloc2("uT", 4 * 2 * 94, BF16)
        uT2 = [(a_[:, 0:4 * 158], r_) for (a_, r_) in uT2raw]
        scrA, R_scrA = A.alloc("scrA", 512)
        qkn, R_qkn = A.alloc("qkn", D, BF16)
        qT2 = alloc2("qT", 1024, BF16)
        kring = [A.alloc(f"kr{i}", 512, BF16) for i in range(6)]
        vring = [A.alloc(f"vr{i}", 8 * 65, BF16) for i in range(6)]
        sg2 = alloc2("sg", 2 * D, BF16)
        ysqb, R_ysq = A.alloc("ysq", 512)
        acc, R_acc = A.alloc("acc", 512)
        R_accs = [R_acc] + [Res(f"acc{i}") for i in range(1, 4)]
        scrB, R_scrB = A.alloc("scrB", D)
        cact2 = alloc2("cact", 512, BF16)
        rowb, R_dg = A.alloc("dg", 256)
        PT, R_PT = A.alloc("PT", 5 * 512, BF16)
        onorm, R_onorm = A.alloc("onorm", 512, BF16)
        rden, R_rden = A.alloc("rden", 8)
        oT2 = alloc2("oT", 512, BF16)
        m1, R_m1 = A.alloc("m1", D)
        merged, R_merged = A.alloc("merged", D, BF16)
        mT, R_mT = A.alloc("mT", D, BF16)
        cst, R_cst = A.alloc("cst", 512)
        cbf, R_cbf = A.alloc("cbf", 512, BF16)
        uTs2 = uT2raw
        hst, R_hst = cst, R_cst
        hbf, R_hbf = cbf, R_cbf
        p1_end = A.ptr

        PTR = banks[0]
        PJ = [banks[1], banks[2], banks[3]]
        pj_rr = [0]

        ctx = {"st": "A"}
        PJB = [banks[5], banks[6], banks[7]]
        SC = [banks[5], banks[6]]
        PO = [banks[7], banks[4]]
        pj_rrb = [0]

        def pjbank():
            if ctx["st"] == "A":
                b = PJ[pj_rr[0] % 3]
                pj_rr[0] += 1
            else:
                b = PJB[pj_rrb[0] % 3]
                pj_rrb[0] += 1
            return b

        def PR():
            return PTR[1] if ctx["st"] == "A" else banks[4][1]

        def ptr_bf(n):
            t_ = PTR[0] if ctx["st"] == "A" else banks[4][0]
            return t_.bitcast(BF16)[:, 0:n]

        R_statn2 = [R_statn, Res("statn1")]

        def norm_load(blk):
            dma(lambda e: e.dma_start(out=xA, in_=blk["x"]), w=[R_xA], dres=R_xA)

        def norm_stats(blk):
            p_ = blk["t"] % 2
            statn = statn8[:, p_ * 4:p_ * 4 + 4]
            R_st = R_statn2[p_]
            op("act", multi(lambda e: e.activation(out=xn, in_=xA, func=AF.Square, accum_out=statn[:, 0:1])),
               r=[R_xA], w=[R_xn, R_st])
            op("pool", lambda e: e.tensor_scalar(out=statn[:, 1:2], in0=statn[:, 0:1], scalar1=1.0 / D, scalar2=EPS,
                                                 op0=ALU.mult, op1=ALU.add), r=[R_st], w=[R_st])
            op("pool", lambda e: e.tensor_tensor(out=statn[:, 2:3], in0=statn[:, 1:2], in1=MHALF, op=ALU.pow),
               r=[R_st, R_onesc], w=[R_st])

        def norm_to_hT(xbuf, R_x, hT, R_hT, mods, Afn, SHfn, p_):
            statn = statn8[:, p_ * 4:p_ * 4 + 4]
            op("act", lambda e: e.activation(out=xn, in_=xbuf, func=AF.Identity, scale=statn[:, 2:3]),
               r=[R_x, R_statn2[p_]], w=[R_xn])
            pt = ptr_bf(1024)
            for c in range(8):
                op("pe", lambda e, c=c: e.transpose(out=pt[:, c * 128:(c + 1) * 128],
                                                    in_=xn[:, c * 128:(c + 1) * 128], identity=ident),
                   r=[R_xn, R_ident], w=[PR()])
            hT_v = hT.rearrange("p (c t) -> p c t", c=8)
            ncol = 128 // len(mods)
            for c in range(8):
                for i, m in enumerate(mods):
                    op("act", lambda e, c=c, i=i, m=m: e.activation(
                        out=hT_v[:, c, i * ncol:(i + 1) * ncol],
                        in_=pt[:, c * 128 + i * ncol:c * 128 + (i + 1) * ncol],
                        func=AF.Identity, scale=Afn(c, m), bias=SHfn(c, m)),
                       r=[PR(), R_modT], w=[R_hT])
            return hT_v

        def proj(hT_v, R_hT, W_v, R_W, n0, n1, kch, bank):
            pb, R_pb = bank
            for k in range(kch):
                op("pe", lambda e, k=k: e.matmul(pb[:, 0:n1 - n0], lhsT=hT_v[:, k, :], rhs=W_v[:, k, n0:n1],
                                                 start=(k == 0), stop=(k == kch - 1)),
                   r=[R_hT, R_W], w=[R_pb])
            return pb, R_pb

        def transpose_to(src, R_src, nchunks, dst_v, R_dst, nrows=128, evac="act", scale=None, r_extra=()):
            pt = ptr_bf(nchunks * 128)
            for c in range(nchunks):
                op("pe", lambda e, c=c: e.transpose(out=pt[:, c * nrows:(c + 1) * nrows],
                                                    in_=src[0:nrows, c * 128:(c + 1) * 128],
                                                    identity=ident[0:nrows, 0:nrows]),
                   r=[R_src, R_ident], w=[PR()])
            pv = pt[:, 0:nchunks * nrows].rearrange("p (c t) -> p c t", c=nchunks)
            if scale is not None:
                op("act", lambda e: e.activation(out=dst_v, in_=pv, func=AF.Identity, scale=scale),
                   r=[PR()] + list(r_extra), w=[R_dst])
            elif evac == "act":
                op("act", lambda e: e.activation(out=dst_v, in_=pv, func=AF.Copy), r=[PR()], w=[R_dst])
            else:
                op("dve", lambda e: e.tensor_copy(out=dst_v, in_=pv), r=[PR()], w=[R_dst])

        def stageA(blk):
            t = blk["t"]
            par = t % 2
            kind = blk["kind"]
            hT, R_hT = hT2[par]
            hT_v = norm_to_hT(xA, R_xA, hT, R_hT, blk["mods"], A1, SH1, par)
            blk["hT_v"], blk["R_hT"] = hT_v, R_hT
            if blk.get("next") is not None:
                norm_load(blk["next"])
            yield
            need_u = kind != "halo" or blk["last_halo"]
            if need_u:
                ga_, R_ga = proj(hT_v, R_hT, Win_v, R_Win, 0, 512, 8, pjbank())
                gb_, R_gb = proj(hT_v, R_hT, Win_v, R_Win, 512, 1024, 8, pjbank())
                op("act", lambda e: e.activation(out=sgl, in_=gb_, func=AF.Sigmoid), r=[R_gb], w=[R_sgl])
                if blk["conv_out"]:
                    op("dve", lambda e: e.tensor_tensor(out=ufp, in0=ga_, in1=sgl, op=ALU.mult),
                       r=[R_ga, R_sgl], w=[R_ufp])
                    op("act", lambda e: e.activation(out=ubf, in_=ufp, func=AF.Copy), r=[R_ufp], w=[R_ubf])
                else:
                    op("dve", lambda e: e.tensor_tensor(out=ubf, in0=ga_, in1=sgl, op=ALU.mult),
                       r=[R_ga, R_sgl], w=[R_ubf])
                yield
                for (dst, r0, r1) in blk["conv_out"]:
                    dma(lambda e, dst=dst, r0=r0, r1=r1: e.dma_start(out=dst, in_=ufp[r0:r1, :]),
                        r=[R_ufp], w=[R_out], dres=R_ufp)
                if kind != "sample":
                    uT, R_uT = uT2[par]
                    uT_v = uT.rearrange("p (c t) -> p c t", c=4)
                    transpose_to(ubf, R_ubf, 4, uT_v[:, :, 30:158], R_uT,
                                 scale=(FLAG if kind == "halo" else ONE), r_extra=[R_onesc])
                    if kind != "halo":
                        uTp, R_uTp = uT2[1 - par]
                        uTp_v = uTp.rearrange("p (c t) -> p c t", c=4)
                        op("pool", lambda e: e.tensor_copy(out=uT_v[:, :, 0:30], in_=uTp_v[:, :, 128:158]),
                           r=[R_uTp], w=[R_uT])
                    blk["uT_v"], blk["R_uT"] = uT_v, R_uT
                else:
                    uTs, R_uTs = uTs2[par]
                    blk["R_uTs"] = R_uTs
                    uTs_v = uTs.rearrange("p (c i t) -> p c i t", c=4, i=2)
                    pt = ptr_bf(512)
                    for c in range(4):
                        op("pe", lambda e, c=c: e.transpose(out=pt[:, c * 128:(c + 1) * 128],
                                                            in_=ubf[:, c * 128:(c + 1) * 128], identity=ident),
                           r=[R_ubf, R_ident], w=[PR()])
                    for i in range(2):
                        op("act", lambda e, i=i: e.activation(
                            out=uTs_v[:, :, i, 30:94],
                            in_=pt.rearrange("p (c t) -> p c t", c=4)[:, :, i * 64:(i + 1) * 64], func=AF.Copy),
                           r=[PR()], w=[R_uTs])
                    for i in range(2):
                        seq = blk["seqs"][i]
                        dma(lambda e, seq=seq: e.dma_start(out=hst[0:30, :], in_=cconv[seq]), w=[R_hst], dres=R_hst)
                        op("pool", lambda e: e.tensor_copy(out=hbf[0:30, :], in_=hst[0:30, :]), r=[R_hst], w=[R_hbf])
                        pt2 = ptr_bf(120)
                        for c in range(4):
                            op("pe", lambda e, c=c: e.transpose(out=pt2[:, c * 30:(c + 1) * 30],
                                                                in_=hbf[0:30, c * 128:(c + 1) * 128],
                                                                identity=ident[0:30, 0:30]),
                               r=[R_hbf, R_ident], w=[PR()])
                        op("act", lambda e, i=i: e.activation(
                            out=uTs_v[:, :, i, 0:30], in_=pt2.rearrange("p (c t) -> p c t", c=4), func=AF.Copy),
                           r=[PR()], w=[R_uTs])
                    blk["uTs_v"] = uTs_v
            blk["u_done"] = True
            if blk.get("next") is not None:
                norm_stats(blk["next"])
            yield
            kb_, R_kb = proj(hT_v, R_hT, Win_v, R_Win, 1536, 2048, 8, pjbank())
            if kind != "halo":
                qb_, R_qb = proj(hT_v, R_hT, Win_v, R_Win, 1024, 1536, 8, pjbank())
            sq = scrA
            nqk = 2 if kind != "halo" else 1
            op("act", lambda e: e.activation(out=sq[:, 0:512], in_=kb_, func=AF.Square), r=[R_kb], w=[R_scrA])
            if kind != "halo":
                op("act", lambda e: e.activation(out=ufp, in_=qb_, func=AF.Square), r=[R_qb], w=[R_ufp])
            yield
            nh = 8 * nqk
            op("dve", lambda e: e.tensor_reduce(out=stat[:, 8:16],
                                                in_=sq[:, 0:512].rearrange("p (h d) -> p h d", d=64),
                                                axis=AX.X, op=ALU.add), r=[R_scrA], w=[R_stat])
            if kind != "halo":
                op("dve", lambda e: e.tensor_reduce(out=stat[:, 16:24],
                                                    in_=ufp.rearrange("p (h d) -> p h d", d=64),
                                                    axis=AX.X, op=ALU.add), r=[R_ufp], w=[R_stat])
            op("pool", lambda e: e.tensor_scalar(out=stat[:, 8:8 + nh], in0=stat[:, 8:8 + nh], scalar1=1.0 / 64,
                                                 scalar2=EPS, op0=ALU.mult, op1=ALU.add), r=[R_stat], w=[R_stat])
            op("pool", lambda e: e.tensor_tensor(out=stat[:, 24:24 + nh], in0=stat[:, 8:8 + nh],
                                                 in1=MHALF.to_broadcast([128, nh]), op=ALU.pow),
               r=[R_stat, R_onesc], w=[R_stat])
            op("dve", lambda e: e.tensor_tensor(
                out=qkn[:, 0:512].rearrange("p (h d) -> p h d", d=64),
                in0=kb_.rearrange("p (h d) -> p h d", d=64),
                in1=stat[:, 24:32].unsqueeze(2).to_broadcast([128, 8, 64]), op=ALU.mult),
               r=[R_kb, R_stat], w=[R_qkn])
            KV = int(os.environ.get("KDBG_KV", "9"))
            if blk["kout"] is not None and KV >= 1:
                kout = scrA[:, 0:512]
                op("dve", lambda e: e.tensor_tensor(
                    out=kout.rearrange("p (h d) -> p h d", d=64),
                    in0=kb_.rearrange("p (h d) -> p h d", d=64),
                    in1=stat[:, 24:32].unsqueeze(2).to_broadcast([128, 8, 64]), op=ALU.mult),
                   r=[R_kb, R_stat], w=[R_scrA])
                op("pool", lambda e: e.tensor_tensor(out=kout, in0=kout, in1=gkbc, op=ALU.mult),
                   r=[R_scrA, R_gkbc], w=[R_scrA])
                if KV >= 2:
                    dma(lambda e: e.dma_start(out=blk["kout"], in_=kout), r=[R_scrA], w=[R_out], dres=R_scrA)
            if kind != "halo":
                op("dve", lambda e: e.tensor_tensor(
                    out=qkn[:, 512:1024].rearrange("p (h d) -> p h d", d=64),
                    in0=qb_.rearrange("p (h d) -> p h d", d=64),
                    in1=stat[:, 32:40].unsqueeze(2).to_broadcast([128, 8, 64]), op=ALU.mult),
                   r=[R_qb, R_stat], w=[R_qkn])
            yield
            kT, R_kT = blk["kslot"]
            kT_v = kT.rearrange("p (c t) -> p c t", c=4)
            transpose_to(qkn[:, 0:512], R_qkn, 4, kT_v, R_kT, scale=gqk_t[:, 1:2], r_extra=[R_gqk])
            if kind != "halo":
                qT, R_qT = qT2[par]
                qT_v = qT.rearrange("p (o c t) -> p o c t", o=2, c=4)
                ptq = ptr_bf(512)
                for c in range(4):
                    op("pe", lambda e, c=c: e.transpose(out=ptq[:, c * 128:(c + 1) * 128],
                                                        in_=qkn[:, 512 + c * 128:512 + (c + 1) * 128], identity=ident),
                       r=[R_qkn, R_ident], w=[PR()])
                for o_ in range(2):
                    op("act", lambda e, o_=o_: e.activation(
                        out=qT_v[:, o_, :, :], in_=ptq.rearrange("p (c t) -> p c t", c=4), func=AF.Identity,
                        scale=gqs_t[:, o_:o_ + 1]), r=[PR(), R_gqs], w=[R_qT])
                blk["qT_v"], blk["R_qT"] = qT_v, R_qT
            yield
            vb_, R_vb = proj(hT_v, R_hT, Win_v, R_Win, 2048, 2560, 8, pjbank())
            vx, R_vx = blk["vslot"]
            vx_v = vx.rearrange("p (h d) -> p h d", d=65)
            fl = FLAG if kind == "halo" else ONE
            op("pool", lambda e: e.tensor_copy(out=vx_v[:, :, 64:65], in_=fl.unsqueeze(2).to_broadcast([128, 8, 1])),
               r=[R_onesc], w=[R_vx])
            op("act", lambda e: e.activation(out=vx_v[:, :, 0:64], in_=vb_.rearrange("p (h d) -> p h d", d=64),
                                             func=AF.Identity, scale=fl), r=[R_vb, R_onesc], w=[R_vx])
            if blk["vout"] is not None and KV >= 3:
                vout = sgl
                if KV != 5:
                    op("act", lambda e: e.activation(out=vout, in_=vb_, func=AF.Identity), r=[R_vb], w=[R_sgl])
                if KV != 4:
                    dma(lambda e: e.dma_start(out=blk["vout"], in_=vout), r=[R_sgl], w=[R_out], dres=R_sgl)
            yield

        def conv_ln(blk):
            acc_v = acc.rearrange("p (c t) -> p c t", c=4)
            sample = blk["kind"] == "sample"

            def src(c, j):
                if sample:
                    return blk["uTs_v"][:, c, :, j:j + 64]
                return blk["uT_v"][:, c, j:j + 128]

            def dst(c):
                if sample:
                    return acc_v[:, c, :].rearrange("p (i t) -> p i t", i=2)
                return acc_v[:, c, :]
            rU = [blk["R_uTs"]] if sample else [blk["R_uT"]]
            for j in range(31):
                if j % CONV_TAPS_PER_SEG == 0 and j > 0:
                    yield
                for c in range(4):
                    if j == 0:
                        op("dve", lambda e, c=c, j=j: e.tensor_scalar(
                            out=dst(c), in0=src(c, j), scalar1=convp_v[:, c, 0:1], scalar2=convp_v[:, c, 31:32],
                            op0=ALU.mult, op1=ALU.add), r=rU + [R_convp], w=[R_accs[c]])
                    else:
                        op("dve", lambda e, c=c, j=j: e.scalar_tensor_tensor(
                            out=dst(c), in0=src(c, j), scalar=convp_v[:, c, j:j + 1], in1=dst(c),
                            op0=ALU.mult, op1=ALU.add), r=rU + [R_convp, R_accs[c]], w=[R_accs[c]])
            return None

        def ln_part(blk):
            par = blk["t"] % 2
            acc_v = acc.rearrange("p (c t) -> p c t", c=4)
            ysq = ysqb
            op("act", lambda e: e.activation(out=ysq, in_=acc, func=AF.Square), r=R_accs, w=[R_ysq])
            pc, R_pc = pjbank()
            for c in range(4):
                op("pe", lambda e, c=c: e.matmul(pc[:, 0:1], lhsT=acc_v[:, c, :], rhs=ONE,
                                                 start=(c == 0), stop=(c == 3)), r=R_accs + [R_onesc], w=[R_pc])
            for c in range(4):
                op("pe", lambda e, c=c: e.matmul(pc[:, 1:2], lhsT=ysq[:, c * 128:(c + 1) * 128], rhs=ONE,
                                                 start=(c == 0), stop=(c == 3)), r=[R_ysq, R_onesc], w=[R_pc])
            lst = statl
            op("dve", lambda e: e.tensor_scalar(out=lst[:, 0:2], in0=pc[:, 0:2], scalar1=1.0 / 512, scalar2=None,
                                                op0=ALU.mult), r=[R_pc], w=[R_statl])
            yield
            op("dve", lambda e: e.tensor_tensor(out=lst[:, 2:3], in0=lst[:, 0:1], in1=lst[:, 0:1], op=ALU.mult),
               r=[R_statl], w=[R_statl])
            op("dve", lambda e: e.scalar_tensor_tensor(out=lst[:, 3:4], in0=lst[:, 1:2], scalar=EPS, in1=lst[:, 2:3],
                                                       op0=ALU.add, op1=ALU.subtract), r=[R_statl], w=[R_statl])
            op("pool", lambda e: e.tensor_tensor(out=lst[:, 4:5], in0=lst[:, 3:4], in1=MHALF, op=ALU.pow),
               r=[R_statl, R_onesc], w=[R_statl])
            op("dve", lambda e: e.scalar_tensor_tensor(out=lst[:, 5:6], in0=lst[:, 0:1], scalar=-1.0, in1=lst[:, 4:5],
                                                       op0=ALU.mult, op1=ALU.mult), r=[R_statl], w=[R_statl])
            dg = rowb[:, 0:256]
            for i in range(2):
                op("dve", lambda e, i=i: e.tensor_scalar(out=dg[:, i * 128:(i + 1) * 128], in0=identf,
                                                         scalar1=lst[:, 4 + i:5 + i], scalar2=None, op0=ALU.mult),
                   r=[R_statl, R_identf], w=[R_dg])
            yield
            pb2, R_pb2 = pjbank()
            op("pe", lambda e: e.matmul(pb2[:, 0:256], lhsT=onesrow, rhs=dg, start=True, stop=True),
               r=[R_dg, R_onesrow], w=[R_pb2])
            op("dve", lambda e: e.tensor_tensor(out=acc_v, in0=acc_v,
                                                in1=pb2[:, 0:128].unsqueeze(1).to_broadcast([128, 4, 128]),
                                                op=ALU.mult), r=R_accs + [R_pb2], w=R_accs)
            op("dve", lambda e: e.tensor_tensor(out=acc_v, in0=acc_v,
                                                in1=pb2[:, 128:256].unsqueeze(1).to_broadcast([128, 4, 128]),
                                                op=ALU.add), r=R_accs + [R_pb2], w=R_accs)
            yield
            cact, R_cact = cact2[par]
            cact_v = cact.rearrange("p (c t) -> p c t", c=4)
            blk["cact_v"], blk["R_cact"] = cact_v, R_cact
            for c in range(4):
                op("act", lambda e, c=c: e.activation(out=cact_v[:, c, :], in_=acc_v[:, c, :], func=AF.Silu,
                                                      scale=convp_v[:, c, 32:33], bias=convp_v[:, c, 33:34]),
                   r=R_accs + [R_convp], w=[R_cact])
            blk["ln_done"] = True
            yield

        def attend(qT_v, R_qT, q0, nq, kblocks, vblocks, tables, onorm_rows):
            PT_v = PT.rearrange("p (k h q) -> p k h q", k=5, h=4)
            DBGA = int(os.environ.get("KDBG_ATT", "9"))
            for hg in range(2):
                for kb in range(5):
                    kT_v, R_k = kblocks[kb]
                    sb_, R_sb = SC[(hg * 5 + kb) % 2]
                    tb, R_tb = tables[kb]
                    for hh in range(4):
                        h = hg * 4 + hh
                        hp, od = h // 2, h % 2
                        op("pe", lambda e, hh=hh, hp=hp, od=od, kT_v=kT_v, sb_=sb_: e.matmul(
                            sb_[:, hh * nq:(hh + 1) * nq], lhsT=kT_v[:, hp, :],
                            rhs=qT_v[:, od, hp, q0:q0 + nq], start=True, stop=False),
                           r=[R_k, R_qT], w=[R_sb])
                        op("pe", lambda e, hh=hh, h=h, tb=tb, sb_=sb_: e.matmul(
                            sb_[:, hh * nq:(hh + 1) * nq], lhsT=ident, rhs=tb[:, h, :], start=False, stop=True),
                           r=[R_tb, R_ident], w=[R_sb])
                    pv = PT_v[:, kb, :, 0:nq]
                    op("act", lambda e, sb_=sb_, pv=pv: e.activation(
                        out=pv, in_=sb_[:, 0:4 * nq].rearrange("p (h q) -> p h q", h=4), func=AF.Exp),
                       r=[R_sb], w=[R_PT])
                    if kb % 2 == 1:
                        yield
                ob, R_ob = PO[hg]
                if DBGA < 3:
                    continue
                for hh in range(4):
                    h = hg * 4 + hh
                    for kb in range(5):
                        vx_v, R_v = vblocks[kb]
                        op("pe", lambda e, hh=hh, h=h, kb=kb, vx_v=vx_v, ob=ob: e.matmul(
                            ob[0:nq, hh * 65:(hh + 1) * 65], lhsT=PT_v[:, kb, hh, 0:nq], rhs=vx_v[:, h, :],
                            start=(kb == 0), stop=(kb == 4)), r=[R_PT, R_v], w=[R_ob])
                ob_v = ob[0:nq, 0:260].rearrange("p (h d) -> p h d", d=65)
                if DBGA < 4:
                    continue
                op("dve", lambda e, ob_v=ob_v, hg=hg: e.reciprocal(
                    out=rden[0:nq, hg * 4:(hg + 1) * 4].unsqueeze(2), in_=ob_v[:, :, 64:65]),
                   r=[R_ob], w=[R_rden])
                op("dve", lambda e, ob_v=ob_v, hg=hg: e.tensor_tensor(
                    out=onorm_rows[:, hg * 256:(hg + 1) * 256].rearrange("p (h d) -> p h d", d=64),
                    in0=ob_v[:, :, 0:64],
                    in1=rden[0:nq, hg * 4:(hg + 1) * 4].unsqueeze(2).to_broadcast([nq, 4, 64]), op=ALU.mult),
                   r=[R_ob, R_rden], w=[R_onorm])
                yield

        def stageS2(blk):
            t = blk["t"]
            par = t % 2
            kind = blk["kind"]
            hT_v, R_hT = blk["hT_v"], blk["R_hT"]
            sg, R_sg = sg2[par]
            oT, R_oT = oT2[par]
            blk["sg"], blk["oT"] = (sg, R_sg), (oT, R_oT)
            for i in range(4):
                gb_, R_gb = proj(hT_v, R_hT, Win_v, R_Win, 2560 + i * 512, 3072 + i * 512, 8, pjbank())
                op("act", lambda e, i=i, gb_=gb_: e.activation(out=sg[:, i * 512:(i + 1) * 512], in_=gb_,
                                                               func=AF.Sigmoid), r=[R_gb], w=[R_sg])
                if i % 2 == 1:
                    yield
            DBGB = int(os.environ.get("KDBG_B", "9"))
            if DBGB < 2:
                return
            oT_v = oT.rearrange("p (c t) -> p c t", c=4)
            tabs = [(expB_v[:, TBL[kb], :, :], R_expB) for kb in range(5)]
            if kind == "main":
                kbl = [(kring[s][0].rearrange("p (c t) -> p c t", c=4), kring[s][1]) for s in blk["kslots"]]
                vbl = [(vring[s][0].rearrange("p (h d) -> p h d", d=65), vring[s][1]) for s in blk["kslots"]]
                yield from attend(blk["qT_v"], blk["R_qT"], 0, 128, kbl, vbl, tabs, onorm)
                transpose_to(onorm, R_onorm, 4, oT_v, R_oT)
                yield
            else:
                for i in range(2):
                    seq = blk["seqs"][i]
                    for kb in range(4):
                        dma(lambda e, seq=seq, kb=kb: e.dma_start(out=cst, in_=ck[seq, kb * 128:(kb + 1) * 128, :]),
                            w=[R_cst], dres=R_cst)
                        op("pool", lambda e: e.tensor_copy(out=cbf, in_=cst), r=[R_cst], w=[R_cbf])
                        kT, R_kT = kring[2 + kb]
                        transpose_to(cbf, R_cbf, 4, kT.rearrange("p (c t) -> p c t", c=4), R_kT)
                        dma(lambda e, seq=seq, kb=kb: e.dma_start(out=cst, in_=cv[seq, kb * 128:(kb + 1) * 128, :]),
                            w=[R_cst], dres=R_cst)
                        vx, R_vx = vring[2 + kb]
                        vx_v = vx.rearrange("p (h d) -> p h d", d=65)
                        op("dve", lambda e, vx_v=vx_v: e.tensor_copy(
                            out=vx_v[:, :, 0:64], in_=cst.rearrange("p (h d) -> p h d", d=64)), r=[R_cst], w=[R_vx])
                        op("pool", lambda e, vx_v=vx_v: e.tensor_copy(
                            out=vx_v[:, :, 64:65], in_=ONE.unsqueeze(2).to_broadcast([128, 8, 1])),
                           r=[R_onesc], w=[R_vx])
                        yield
                    sl = blk["kslots"][-1]
                    kbl = [(kring[s][0].rearrange("p (c t) -> p c t", c=4), kring[s][1]) for s in (2, 3, 4, 5, sl)]
                    vbl = [(vring[s][0].rearrange("p (h d) -> p h d", d=65), vring[s][1]) for s in (2, 3, 4, 5, sl)]
                    tb = list(tabs)
                    tb[4] = (expB_v[:, 3, :, 0:64], R_expB) if i == 0 else (expBs_v, R_expBs)
                    tb = [(tt[:, :, 0:64], rr) for (tt, rr) in tb[:4]] + [tb[4]]
                    yield from attend(blk["qT_v"], blk["R_qT"], i * 64, 64, kbl, vbl, tb, onorm[0:64, :])
                    pt = ptr_bf(512)
                    for c in range(4):
                        op("pe", lambda e, c=c, pt=pt: e.transpose(
                            out=pt[:, c * 64:(c + 1) * 64], in_=onorm[0:64, c * 128:(c + 1) * 128],
                            identity=ident[0:64, 0:64]), r=[R_onorm, R_ident], w=[PR()])
                    op("act", lambda e, i=i, pt=pt: e.activation(
                        out=oT_v[:, :, i * 64:(i + 1) * 64],
                        in_=pt[:, 0:256].rearrange("p (c t) -> p c t", c=4), func=AF.Copy),
                       r=[PR()], w=[R_oT])
                    yield

        def stageS3(blk):
            t = blk["t"]
            par = t % 2
            kind = blk["kind"]
            sg, R_sg = blk["sg"]
            oT, R_oT = blk["oT"]
            oT_v = oT.rearrange("p (c t) -> p c t", c=4)
            R_cact = blk["R_cact"]
            if kind == "sample" and not wo_reloaded[0]:
                wo_reloaded[0] = True
                for k in range(8):
                    dma(lambda e, k=k: e.dma_start(out=xA, in_=w_o[k * 128:(k + 1) * 128, :]), w=[R_xA], dres=R_xA)
                    if k % 2 == 0:
                        op("dve", lambda e, k=k: e.tensor_copy(out=Wo_v[:, k, :], in_=xA), r=[R_xA], w=[R_Wo])
                    else:
                        op("act", lambda e, k=k: e.activation(out=Wo_v[:, k, :], in_=xA, func=AF.Copy),
                           r=[R_xA], w=[R_Wo])
                yield
            dma(lambda e: e.dma_start(out=xB, in_=blk["x"]), w=[R_xB], dres=R_xB)
            cact_v = blk["cact_v"]
            for half in range(2):
                cb_, R_cb = proj(cact_v, R_cact, Wco_v, R_Wco, half * 512, (half + 1) * 512, 4, pjbank())
                op("dve", lambda e, half=half, cb_=cb_: e.tensor_tensor(
                    out=m1[:, half * 512:(half + 1) * 512], in0=cb_, in1=sg[:, half * 512:(half + 1) * 512],
                    op=ALU.mult), r=[R_cb, R_sg], w=[R_m1])
                yield
            for half in range(2):
                ab_, R_ab = proj(oT_v, R_oT, Wao_v, R_Wao, half * 512, (half + 1) * 512, 4, pjbank())
                tmp = scrB[:, half * 512:(half + 1) * 512]
                op("dve", lambda e, half=half, ab_=ab_, tmp=tmp: e.tensor_tensor(
                    out=tmp, in0=ab_, in1=sg[:, 1024 + half * 512:1024 + (half + 1) * 512], op=ALU.mult),
                   r=[R_ab, R_sg], w=[R_scrB])
                op("dve", lambda e, half=half, tmp=tmp: e.tensor_tensor(
                    out=merged[:, half * 512:(half + 1) * 512], in0=tmp, in1=m1[:, half * 512:(half + 1) * 512],
                    op=ALU.add), r=[R_scrB, R_m1], w=[R_merged])
                yield
            mT_v = mT.rearrange("p (c t) -> p c t", c=8)
            transpose_to(merged, R_merged, 8, mT_v, R_mT)
            yield
            for half in range(2):
                ob_, R_ob = proj(mT_v, R_mT, Wo_v, R_Wo, half * 512, (half + 1) * 512, 8, pjbank())
                x1h = scrB[:, half * 512:(half + 1) * 512]
                if kind == "sample":
                    gt, R_gt = blk["gt1"]
                    op("dve", lambda e, half=half, ob_=ob_, x1h=x1h, gt=gt: e.tensor_tensor(
                        out=x1h, in0=ob_, in1=gt[:, half * 512:(half + 1) * 512], op=ALU.mult),
                       r=[R_ob, R_gt], w=[R_scrB])
                    op("dve", lambda e, half=half, x1h=x1h: e.tensor_tensor(
                        out=x1h, in0=x1h, in1=xB[:, half * 512:(half + 1) * 512], op=ALU.add),
                       r=[R_scrB, R_xB], w=[R_scrB])
                else:
                    op("dve", lambda e, half=half, ob_=ob_, x1h=x1h: e.tensor_tensor(
                        out=x1h, in0=ob_, in1=xB[:, half * 512:(half + 1) * 512], op=ALU.add),
                       r=[R_ob, R_xB], w=[R_scrB])
                yield
            xi = blk["x1idx"]
            dma(lambda e: e.dma_start(out=x1s[xi * 128:(xi + 1) * 128, :], in_=scrB), r=[R_scrB], w=[R_x1s[xi]],
                dres=R_scrB)

        blocks = []
        for bi in range(NHB + NPB):
            halo = bi < NHB
            b = {"t": bi, "kind": "halo" if halo else "main", "last_halo": bi == NHB - 1,
                 "x": xp[bi * 128:(bi + 1) * 128, :], "mods": [0],
                 "kslot": kring[bi % 6], "vslot": vring[bi % 6],
                 "kslots": [(bi - 4 + i) % 6 for i in range(5)],
                 "kout": None, "vout": None, "conv_out": [], "gt1": None, "x1idx": bi - NHB}
            if bi >= NHB + NPB - 4:
                r0 = (bi - (NHB + NPB - 4)) * 128
                b["kout"] = k_p[r0:r0 + 128, :]
                b["vout"] = v_p[r0:r0 + 128, :]
            if bi == NHB + NPB - 1:
                b["conv_out"] = [(conv_p[:, :], 98, 128)]
            blocks.append(b)
        for sbi in range(2):
            bi = NHB + NPB + sbi
            sl = bi % 6
            b = {"t": bi, "kind": "sample", "last_halo": False, "x": xs[sbi * 128:(sbi + 1) * 128, :],
                 "mods": [1 + 2 * sbi, 2 + 2 * sbi], "kslot": kring[sl], "vslot": vring[sl], "kslots": [sl],
                 "seqs": [2 * sbi, 2 * sbi + 1],
                 "kout": k_s[sbi * 128:(sbi + 1) * 128, :], "vout": v_s[sbi * 128:(sbi + 1) * 128, :],
                 "conv_out": [(conv_s[2 * sbi], 34, 64), (conv_s[2 * sbi + 1], 98, 128)],
                 "gt1": gt1s[sbi], "x1idx": NPB + sbi}
            blocks.append(b)

        nb = len(blocks)
        for i_ in range(nb - 1):
            blocks[i_]["next"] = blocks[i_ + 1]
        norm_load(blocks[0])
        norm_stats(blocks[0])
        DBG_STEPS = int(os.environ.get("KDBG_STEPS", "-1"))
        DBG_P2 = int(os.environ.get("KDBG_P2", "-1"))
        if DBG_STEPS >= 0:
            nb = DBG_STEPS - 1
        wo_reloaded = [False]

        def gS2(blk):
            yield from ln_part(blk)
            yield from stageS2(blk)

        for step in range(nb + 2):
            chains = []
            if step < nb:
                chains.append(["A", stageA(blocks[step]), "A"])
            b2 = blocks[step - 1] if 1 <= step <= nb and blocks[step - 1]["kind"] != "halo" else None
            b3 = blocks[step - 2] if 2 <= step <= nb + 1 and blocks[step - 2]["kind"] != "halo" else None
            if os.environ.get("KDBG_NOB"):
                b2 = b3 = None
            if b2 is not None:
                chains.append(["B", gS2(b2), "S2"])
            if b3 is not None:
                chains.append(["B", stageS3(b3), "S3"])
            chains.sort(key=lambda it: CHAIN_ORDER.index(it[2]))
            want_conv = step < nb and blocks[step]["kind"] != "halo"
            conv_added = False
            while chains or (want_conv and not conv_added):
                for item in list(chains):
                    if item not in chains:
                        continue
                    ctx["st"] = item[0]
                    try:
                        next(item[1])
                    except StopIteration:
                        chains.remove(item)
                    if CONV_BOOST and item[2] != "C":
                        for cch_ in [c_ for c_ in chains if c_[2] == "C"]:
                            ctx["st"] = cch_[0]
                            try:
                                next(cch_[1])
                            except StopIteration:
                                chains.remove(cch_)
                a_done = blocks[step].get("u_done") if step < nb else True
                ln_ok = b2 is None or b2.get("ln_done")
                if want_conv and not conv_added and a_done and ln_ok:
                    chains.append(["A", conv_ln(blocks[step]), "C"])
                    conv_added = True
        ctx["st"] = "A"

        A.ptr = const_end
        Wf1, R_Wf1 = A.alloc("Wf1", 8 * 2 * FFN, BF16)
        Wf1_v = Wf1.rearrange("p (k n) -> p k n", k=8)
        Wf2, R_Wf2 = A.alloc("Wf2", 22 * D, BF16)
        Wf2_v = Wf2.rearrange("p (k n) -> p k n", k=22)
        gt2p, R_gt2p = A.alloc("gt2p", D)
        gt2s = [A.alloc(f"gt2s{i}", D) for i in range(2)]
        wst2 = [A.alloc(f"wstb{i}", 2304) for i in range(3)]
        wst[:] = wst2
        cast_queues[0] = ("sp", "pool")
        load_cast(w_ffn_in, D, 2 * FFN, Wf1_v, R_Wf1, piece=2304)
        load_cast(w_ffn_out, FFN, D, Wf2_v, R_Wf2, piece=1024)
        load_gt(5 * D, [(gt2p, R_gt2p, [(0, 128, 0)]),
                        (gt2s[0][0], gt2s[0][1], [(0, 64, 1), (64, 128, 2)]),
                        (gt2s[1][0], gt2s[1][1], [(0, 64, 3), (64, 128, 4)])])
        A.ptr -= 3 * 2304
        x1b = alloc2("x1b", D)
        xn2, R_xn2 = A.alloc("xn2", D, BF16)
        h2 = alloc2("h2T", D, BF16)
        stat2, R_stat2 = A.alloc("stat2", 8)
        sil = alloc2("sil", 512)
        actb, R_actb = A.alloc("actb", FFN, BF16)
        aT = alloc2("aT", 22 * 128, BF16)
        ALLB = banks[1:8]
        rr2 = [0]

        def bank2():
            b = ALLB[rr2[0] % 7]
            rr2[0] += 1
            return b

        widths = [(0, 512), (512, 1024), (1024, 1536), (1536, 2048), (2048, 2560), (2560, 2816)]
        fblocks = [(i, y_p[i * 128:(i + 1) * 128, :], [0], (gt2p, R_gt2p)) for i in range(NPB)]
        fblocks += [(NPB + i, y_s[i * 128:(i + 1) * 128, :], [1 + 2 * i, 2 + 2 * i], gt2s[i]) for i in range(2)]

        def ffnA(fb):
            idx, ydst, mods, gt = fb
            par = idx % 2
            xb, R_xb = x1b[par]
            dma(lambda e: e.dma_start(out=xb, in_=x1s[idx * 128:(idx + 1) * 128, :]), r=[R_x1s[idx]], w=[R_xb],
                dres=R_xb)
            hT, R_hT = h2[par]
            op("act", multi(lambda e: e.activation(out=xn2, in_=xb, func=AF.Square, accum_out=stat2[:, 0:1])),
               r=[R_xb], w=[R_xn2, R_stat2])
            op("pool", lambda e: e.tensor_scalar(out=stat2[:, 1:2], in0=stat2[:, 0:1], scalar1=1.0 / D, scalar2=EPS,
                                                 op0=ALU.mult, op1=ALU.add), r=[R_stat2], w=[R_stat2])
            op("pool", lambda e: e.tensor_tensor(out=stat2[:, 2:3], in0=stat2[:, 1:2], in1=MHALF, op=ALU.pow),
               r=[R_stat2, R_onesc], w=[R_stat2])
            op("act", lambda e: e.activation(out=xn2, in_=xb, func=AF.Identity, scale=stat2[:, 2:3]),
               r=[R_xb, R_stat2], w=[R_xn2])
            pt = ptr_bf(1024)
            for c in range(8):
                op("pe", lambda e, c=c: e.transpose(out=pt[:, c * 128:(c + 1) * 128],
                                                    in_=xn2[:, c * 128:(c + 1) * 128], identity=ident),
                   r=[R_xn2, R_ident], w=[PR()])
            hT_v = hT.rearrange("p (c t) -> p c t", c=8)
            ncol = 128 // len(mods)
            for c in range(8):
                for i, m in enumerate(mods):
                    op("act", lambda e, c=c, i=i, m=m: e.activation(
                        out=hT_v[:, c, i * ncol:(i + 1) * ncol],
                        in_=pt[:, c * 128 + i * ncol:c * 128 + (i + 1) * ncol],
                        func=AF.Identity, scale=A2(c, m), bias=SH2(c, m)), r=[PR(), R_modT], w=[R_hT])
            return hT_v, R_hT

        R_actw = [Res(f"actw{i}") for i in range(6)]
        aT_res = [[Res(f"aT{p}_{i}") for i in range(6)] for p in range(2)]
        accb = [banks[1], banks[2]]
        PAIRB = [banks[3], banks[4], banks[5], banks[6], banks[7]]
        pair_rr = [0]

        def pbank():
            b_ = PAIRB[pair_rr[0] % 5]
            pair_rr[0] += 1
            return b_

        def ffnB(fb, hT_v, R_hT, next_fb):
            idx, ydst, mods, gt = fb
            par = idx % 2
            xb, R_xb = x1b[par]
            aTt, R_aT0 = aT[par]
            aT_v = aTt.rearrange("p (c t) -> p c t", c=22)
            RaT = aT_res[par]
            nxt = None

            def tr(wi):
                n0, n1 = widths[wi]
                c0, nch = n0 // 128, (n1 - n0) // 128
                pt = ptr_bf(nch * 128)
                for c in range(nch):
                    op("pe", lambda e, c=c, c0=c0, pt=pt: e.transpose(
                        out=pt[:, c * 128:(c + 1) * 128], in_=actb[:, (c0 + c) * 128:(c0 + c + 1) * 128],
                        identity=ident), r=[R_actw[wi], R_ident], w=[PR()])
                pv = pt.rearrange("p (c t) -> p c t", c=nch)
                wr = [RaT[wi]] + ([R_aT0] if wi == 0 else [])
                if wi % 2 == 0:
                    op("act", lambda e, pv=pv, c0=c0, nch=nch: e.activation(out=aT_v[:, c0:c0 + nch, :], in_=pv,
                                                                            func=AF.Copy), r=[PR()], w=wr)
                else:
                    op("dve", lambda e, pv=pv, c0=c0, nch=nch: e.tensor_copy(out=aT_v[:, c0:c0 + nch, :], in_=pv),
                       r=[PR()], w=wr)

            def mm2(wi):
                n0, n1 = widths[wi]
                for c in range(n0 // 128, n1 // 128):
                    for half in range(2):
                        ab_, R_ab = accb[half]
                        op("pe", lambda e, c=c, half=half, ab_=ab_: e.matmul(
                            ab_[:, 0:512], lhsT=aT_v[:, c, :], rhs=Wf2_v[:, c, half * 512:(half + 1) * 512],
                            start=(c == 0), stop=(c == 21)), r=[RaT[wi], R_Wf2], w=[R_ab])

            for wi, (n0, n1) in enumerate(widths):
                gb_, R_gb = proj(hT_v, R_hT, Wf1_v, R_Wf1, n0, n1, 8, pbank())
                ub_, R_ub = proj(hT_v, R_hT, Wf1_v, R_Wf1, FFN + n0, FFN + n1, 8, pbank())
                sl_, R_sl = sil[wi % 2]
                op("act", lambda e, gb_=gb_, sl_=sl_, n0=n0, n1=n1: e.activation(
                    out=sl_[:, 0:n1 - n0], in_=gb_[:, 0:n1 - n0], func=AF.Silu), r=[R_gb], w=[R_sl])
                op("dve", lambda e, ub_=ub_, sl_=sl_, n0=n0, n1=n1: e.tensor_tensor(
                    out=actb[:, n0:n1], in0=ub_[:, 0:n1 - n0], in1=sl_[:, 0:n1 - n0], op=ALU.mult),
                   r=[R_ub, R_sl], w=[R_actw[wi]])
                if wi >= 1:
                    tr(wi - 1)
                if wi >= 2:
                    mm2(wi - 2)
                if wi == 2 and next_fb is not None:
                    nxt = (next_fb,) + ffnA(next_fb)
            tr(5)
            mm2(4)
            mm2(5)
            gtt, R_gt = gt
            for half in range(2):
                ob_, R_ob = accb[half]
                tmp_, R_tmp = sil[half]
                yh = xb[:, half * 512:(half + 1) * 512]
                op("dve", lambda e, half=half, ob_=ob_, tmp_=tmp_: e.tensor_tensor(
                    out=tmp_, in0=ob_, in1=gtt[:, half * 512:(half + 1) * 512], op=ALU.mult),
                   r=[R_ob, R_gt], w=[R_tmp])
                op("dve", lambda e, half=half, yh=yh, tmp_=tmp_: e.tensor_tensor(
                    out=yh, in0=yh, in1=tmp_, op=ALU.add),
                   r=[R_tmp, R_xb], w=[R_xb])
            dma(lambda e: e.dma_start(out=ydst, in_=xb), r=[R_xb], w=[R_out], dres=R_xb)
            return nxt

        if DBG_P2 >= 0:
            fblocks = fblocks[:DBG_P2]
        if DBG_STEPS >= 0 and DBG_P2 < 0:
            fblocks = []
        cur = None
        if fblocks:
            cur = (fblocks[0],) + ffnA(fblocks[0])
        for i_, fb in enumerate(fblocks):
            nfb = fblocks[i_ + 1] if i_ + 1 < len(fblocks) else None
            cur = ffnB(cur[0], cur[1], cur[2], nfb)

        S.emit()
    return nc


def _prep_inputs(inp):
    f = np.float32
    x_prompt = np.asarray(inp["x_prompt"], f)
    x_sample = np.asarray(inp["x_sample"], f)
    c_prompt = np.asarray(inp["c_prompt"], f)
    c_sample = np.asarray(inp["c_sample"], f)
    cache_conv = np.asarray(inp["cache_conv"], f)[0]
    cache_k = np.asarray(inp["cache_k"], f)[0].reshape(32, 512, 512)
    cache_v = np.asarray(inp["cache_v"], f)[0].reshape(32, 512, 512)
    rel_bias = np.asarray(inp["rel_bias"], f)[0]
    key = np.arange(128)[:, None]
    q = np.arange(128)[None, :]
    tabs = []
    for kb in (0, 1, 3, 4):
        ridx = np.clip(512 + q - 128 * kb - key, -128, 128) + 128
        tabs.append(np.transpose(rel_bias[:, ridx], (1, 0, 2)).reshape(128, 8 * 128))
    relbT = np.ascontiguousarray(np.stack(tabs, 0))
    w_dw = np.asarray(inp["w_dw"], f)[0]
    convp = np.concatenate([w_dw.T, np.asarray(inp["b_dw"], f)[0][:, None],
                            np.asarray(inp["conv_ln_g"], f)[0][:, None],
                            np.asarray(inp["conv_ln_b"], f)[0][:, None]], axis=1)
    convp = np.ascontiguousarray(convp.reshape(4, 128, 34).transpose(1, 0, 2).reshape(128, 4 * 34))
    g1 = np.asarray(inp["norm1_g"], f)[0].reshape(8, 128).T
    g2 = np.asarray(inp["norm2_g"], f)[0].reshape(8, 128).T
    gfm = np.ascontiguousarray(np.concatenate([g1, g2], axis=1))
    gq = np.asarray(inp["q_norm_g"], f)[0]
    gk = np.asarray(inp["k_norm_g"], f)[0]
    gqk = np.ascontiguousarray(np.stack([np.tile(gq, 2), np.tile(gk, 2)], axis=1))
    gkrow = np.ascontiguousarray(np.tile(gk, 8)[None, :])
    shared = {
        "relbT": relbT, "w_ada": np.asarray(inp["w_ada"], f)[0], "b_ada": np.asarray(inp["b_ada"], f),
        "gfm": gfm, "w_in": np.asarray(inp["w_in"], f)[0], "convp": convp,
        "w_conv_out": np.asarray(inp["w_conv_out"], f)[0], "gqk": gqk, "gkrow": gkrow,
        "w_attn_out": np.asarray(inp["w_attn_out"], f)[0], "w_o": np.asarray(inp["w_o"], f)[0],
        "w_ffn_in": np.asarray(inp["w_ffn_in"], f)[0], "w_ffn_out": np.asarray(inp["w_ffn_out"], f)[0],
    }
    in_maps = []
    for core in range(8):
        b, seg = core // 4, core % 4
        start = seg * 4096
        xpc = np.zeros((NHB * 128 + NPB * 128, D), f)
        if seg > 0:
            xpc[:] = x_prompt[b, start - 512:start + 4096]
        else:
            xpc[512:] = x_prompt[b, 0:4096]
        m = dict(shared)
        m["xp"] = xpc
        m["xs"] = np.ascontiguousarray(x_sample[4 * core:4 * core + 4].reshape(256, D))
        m["cT"] = np.ascontiguousarray(
            np.stack([c_prompt[b]] + [c_sample[4 * core + i] for i in range(4)], axis=1))
        m["flag"] = np.full((128, 1), 1.0 if seg > 0 else 0.0, f)
        m["cconv"] = np.ascontiguousarray(cache_conv[4 * core:4 * core + 4])
        m["ck"] = np.ascontiguousarray(cache_k[4 * core:4 * core + 4])
        m["cv"] = np.ascontiguousarray(cache_v[4 * core:4 * core + 4])
        in_maps.append(m)
    return in_maps


_NC_CACHE = {}


def kernel(**inp):
    if "nc" not in _NC_CACHE:
        _NC_CACHE["nc"] = build()
    nc = _NC_CACHE["nc"]
    in_maps = _prep_inputs(inp)
    res = run_bass_kernel_spmd(nc, in_maps, core_ids=list(range(8)))
    r = res.results
    f = np.float32
    y_prompt = np.stack([np.concatenate([r[b * 4 + s]["y_p"] for s in range(4)], 0) for b in range(2)], 0)
    y_sample = np.concatenate([r[c]["y_s"].reshape(4, 64, D) for c in range(8)], 0)
    conv_p = np.stack([r[3]["conv_p"], r[7]["conv_p"]], 0)[None]
    k_p = np.stack([r[3]["k_p"], r[7]["k_p"]], 0).reshape(1, 2, 512, 8, 64)
    v_p = np.stack([r[3]["v_p"], r[7]["v_p"]], 0).reshape(1, 2, 512, 8, 64)
    conv_s = np.concatenate([r[c]["conv_s"] for c in range(8)], 0)[None]
    k_s = np.concatenate([r[c]["k_s"].reshape(4, 64, 8, 64) for c in range(8)], 0)[None]
    v_s = np.concatenate([r[c]["v_s"].reshape(4, 64, 8, 64) for c in range(8)], 0)[None]
    outs = (y_prompt, y_sample, conv_p, k_p, v_p, conv_s, k_s, v_s)
    return tuple(np.ascontiguousarray(o, dtype=f) for o in outs)
```

```python
import os
import numpy as np
import concourse.bass as bass
import concourse.mybir as mybir
from concourse.bass_utils import run_bass_kernel_spmd
from contextlib import ExitStack

F32 = mybir.dt.float32
BF16 = mybir.dt.bfloat16
AF = mybir.ActivationFunctionType
ALU = mybir.AluOpType
AX = mybir.AxisListType

D = 1024
NPB = 32
NHB = 4
EPS = 1e-6
FFN = 2816
ARENA_WORDS = 53184
CONV_TAPS_PER_SEG = 1
CONV_BOOST = True
CHAIN_ORDER = ("S2", "A", "S3")


ENGS = ("pe", "act", "dve", "pool", "sp")
SEM_EPOCH = 30000
FUSE_WAITS = True


class Res:
    __slots__ = ("name", "last_w", "readers", "dsem", "dcount", "aliases", "excl")

    def __init__(self, name):
        self.name = name
        self.last_w = None
        self.readers = []
        self.dsem = None
        self.dcount = 0
        self.aliases = []
        self.excl = False


class Op:
    __slots__ = ("eng", "fn", "deps", "needs_sig", "sig", "is_dma", "dres",
                 "dtarget", "gidx", "dma_waits", "eidx")

    def __init__(self, eng, fn, is_dma, gidx):
        self.eng = eng
        self.fn = fn
        self.deps = []
        self.needs_sig = False
        self.sig = None
        self.is_dma = is_dma
        self.dres = None
        self.dtarget = 0
        self.gidx = gidx
        self.dma_waits = []


class Sched:
    def __init__(self, nc):
        self.nc = nc
        self.ops = {e: [] for e in ENGS}
        self.n = 0
        self.dma_res = []

    def res(self, name):
        return Res(name)

    def op(self, eng, fn, reads=(), writes=(), dma=False, dres=None):
        o = Op(eng, fn, dma, self.n)
        o.eidx = len(self.ops[eng])
        self.n += 1
        deps = {}

        def add(d, kind):
            if d is o:
                return
            k = id(d)
            if k in deps:
                if kind == "raw":
                    deps[k] = (d, "raw")
            else:
                deps[k] = (d, kind)

        for r in reads:
            if r.last_w is not None:
                add(r.last_w, "raw")
            if r.excl:
                for rd in r.readers:
                    if rd.eng != eng:
                        add(rd, "war")
        for w in writes:
            if w.last_w is not None:
                add(w.last_w, "waw")
            for rd in w.readers:
                add(rd, "war")
            if w.aliases:
                for a in w.aliases:
                    if a.last_w is not None:
                        add(a.last_w, "waw")
                    for rd in a.readers:
                        add(rd, "war")
                w.aliases = []
        for d, kind in deps.values():
            if d.is_dma:
                o.dma_waits.append((d.dres, d.dres.dcount))
            else:
                if d.eng == o.eng and not o.is_dma:
                    if o.eng == "pe":
                        continue
                d.needs_sig = True
                o.deps.append(d)
        for r in reads:
            if not o.is_dma:
                r.readers = [x for x in r.readers if x.is_dma or x.eng != o.eng]
            r.readers.append(o)
        for w in writes:
            w.last_w = o
            w.readers = []
        if dma:
            assert dres is not None
            if dres.dsem is None:
                self.dma_res.append(dres)
                dres.dsem = True
            dres.dcount += 16
            o.dres = dres
            o.dtarget = dres.dcount
        self.ops[eng].append(o)
        return o

    def emit(self, extra_ctx=()):
        nc = self.nc
        from contextlib import ExitStack
        nsem = {}
        for e in ENGS:
            c = 0
            for o in self.ops[e]:
                if o.needs_sig:
                    o.sig = (c // SEM_EPOCH, c % SEM_EPOCH + 1)
                    c += 1
            nsem[e] = max(1, (c + SEM_EPOCH - 1) // SEM_EPOCH)
        with ExitStack() as st:
            esem = {e: [st.enter_context(nc.semaphore(f"s_{e}{i}")) for i in range(nsem[e])]
                    for e in ENGS}
            for r in self.dma_res:
                r.dsem = st.enter_context(nc.semaphore(f"d_{r.name}"))
            block = st.enter_context(nc.Block())
            handles = {"pe": block.tensor, "act": block.scalar, "dve": block.vector,
                       "pool": block.gpsimd, "sp": block.sync}
            final_dma = [(r.dsem, r.dcount) for r in self.dma_res]

            def make(e):
                ops = self.ops[e]

                def body(eng):
                    waited = {}
                    for o in ops:
                        ws = []
                        for d in o.deps:
                            ep, v = d.sig
                            ws.append((esem[d.eng][ep], v))
                        for (r, v) in o.dma_waits:
                            ws.append((r.dsem, v))
                        need = []
                        for (s, v) in ws:
                            k = id(s)
                            if waited.get(k, 0) >= v:
                                continue
                            waited[k] = v
                            need.append((s, v))
                        fuse = None
                        if need and FUSE_WAITS and not o.is_dma and not getattr(o.fn, "multi", False):
                            fuse = need.pop()
                        for (s, v) in need:
                            eng.wait_ge(s, v)
                        ins = o.fn(eng)
                        if fuse is not None:
                            ins._wait_ge(fuse[0], fuse[1])
                        if o.is_dma:
                            ins.then_inc(o.dres.dsem, 16)
                        elif o.needs_sig:
                            ins.then_inc(esem[e][o.sig[0]], 1)
                    if e == "sp":
                        for (s, v) in final_dma:
                            if waited.get(id(s), 0) < v:
                                eng.wait_ge(s, v)
                return body

            for e in ENGS:
                if self.ops[e] or e == "sp":
                    handles[e](make(e))


class Arena:
    def __init__(self, tensor, nwords):
        self.t = tensor
        self.n = nwords
        self.ptr = 0
        self.all = []

    def alloc(self, name, nelem, dt=F32):
        words = nelem if dt == F32 else (nelem + 1) // 2
        start, end = self.ptr, self.ptr + words
        assert end <= self.n, f"arena overflow at {name}: {end} > {self.n}"
        self.ptr = end
        res = Res(name)
        res.aliases = [r for (s, e, r) in self.all if s < end and e > start]
        self.all.append((start, end, res))
        ap = self.t[:, start:end]
        if dt == BF16:
            ap = ap.bitcast(BF16)[:, :nelem]
        return ap, res


def multi(fn):
    fn.multi = True
    return fn


def build():
    nc = bass.Bass("TRN2", target_bir_lowering=False)

    def din(name, shape):
        return nc.dram_tensor(name, list(shape), F32, kind="ExternalInput").ap()

    def dout(name, shape):
        return nc.dram_tensor(name, list(shape), F32, kind="ExternalOutput").ap()

    xp = din("xp", [NHB * 128 + NPB * 128, D])
    xs = din("xs", [256, D])
    cT = din("cT", [D, 5])
    flag_d = din("flag", [128, 1])
    cconv = din("cconv", [4, 30, 512])
    ck = din("ck", [4, 512, 512])
    cv = din("cv", [4, 512, 512])
    relbT = din("relbT", [4, 128, 8 * 128])
    w_ada = din("w_ada", [D, 6 * D])
    b_ada = din("b_ada", [1, 6 * D])
    gfm = din("gfm", [128, 16])
    w_in = din("w_in", [D, 4608])
    convp = din("convp", [128, 4 * 34])
    w_conv_out = din("w_conv_out", [512, D])
    gqk = din("gqk", [128, 2])
    gkrow = din("gkrow", [1, 512])
    w_attn_out = din("w_attn_out", [512, D])
    w_o = din("w_o", [D, D])
    w_ffn_in = din("w_ffn_in", [D, 2 * FFN])
    w_ffn_out = din("w_ffn_out", [FFN, D])

    y_p = dout("y_p", [NPB * 128, D])
    y_s = dout("y_s", [256, D])
    conv_p = dout("conv_p", [30, 512])
    k_p = dout("k_p", [512, 512])
    v_p = dout("v_p", [512, 512])
    conv_s = dout("conv_s", [4, 30, 512])
    k_s = dout("k_s", [256, 512])
    v_s = dout("v_s", [256, 512])

    modrows = nc.dram_tensor("modrows", [5, 6 * D], F32).ap()
    x1s = nc.dram_tensor("x1s", [(NPB + 2) * 128, D], F32).ap()

    S = Sched(nc)
    with ExitStack() as st:
        arena_t = st.enter_context(nc.sbuf_tensor("arena", [128, ARENA_WORDS], F32))
        ps_t = st.enter_context(nc.psum_tensor("ps", [128, 4096], F32))
        A = Arena(arena_t, ARENA_WORDS)
        banks = [(ps_t[:, 512 * i:512 * (i + 1)], Res(f"bank{i}")) for i in range(8)]
        for _, _r in banks:
            _r.excl = True
        R_modrows = Res("modrows")
        R_x1s = [Res(f"x1s{i}") for i in range(NPB + 2)]
        R_out = Res("outs")

        def op(eng, fn, r=(), w=()):
            return S.op(eng, fn, reads=r, writes=w)

        def dma(fn, r=(), w=(), dres=None, q="sp"):
            return S.op(q, fn, reads=r, writes=w, dma=True, dres=dres)

        identf, R_identf = A.alloc("identf", 128)
        ident, R_ident = A.alloc("ident", 128, BF16)
        onesc, R_onesc = A.alloc("onesc", 8)
        onesrow, R_onesrow = A.alloc("onesrow", 128)
        op("pool", lambda e: e.memset(identf, 1.0), w=[R_identf])
        op("pool", lambda e: e.affine_select(out=identf, in_=identf, pattern=[[-1, 128]],
                                             compare_op=ALU.is_equal, fill=0.0, base=0,
                                             channel_multiplier=1), r=[R_identf], w=[R_identf])
        op("dve", lambda e: e.tensor_copy(out=ident, in_=identf), r=[R_identf], w=[R_ident])
        op("pool", lambda e: e.memset(onesc[:, 0:1], 1.0), w=[R_onesc])
        op("pool", lambda e: e.memset(onesc[:, 1:2], -0.5), w=[R_onesc])
        op("pool", lambda e: e.memset(onesrow, 1.0), w=[R_onesrow])
        dma(lambda e: e.dma_start(out=onesc[:, 2:3], in_=flag_d), w=[R_onesc], dres=R_onesc)
        ONE = onesc[:, 0:1]
        MHALF = onesc[:, 1:2]
        FLAG = onesc[:, 2:3]

        gfm_t, R_gfm = A.alloc("gfm", 16)
        convp_t, R_convp = A.alloc("convp", 4 * 34)
        gqk_t, R_gqk = A.alloc("gqk", 2)
        gqs_t, R_gqs = A.alloc("gqs", 2)
        gkbc, R_gkbc = A.alloc("gkbc", 512)
        dma(lambda e: e.dma_start(out=gfm_t, in_=gfm), w=[R_gfm], dres=R_gfm)
        dma(lambda e: e.dma_start(out=convp_t, in_=convp), w=[R_convp], dres=R_convp)
        dma(lambda e: e.dma_start(out=gqk_t, in_=gqk), w=[R_gqk], dres=R_gqk)
        dma(lambda e: e.dma_start(out=gkbc, in_=gkrow.partition_broadcast(128)), w=[R_gkbc], dres=R_gkbc)
        for col in range(2):
            op("dve", lambda e, col=col: e.tensor_scalar(out=gqs_t[:, col:col + 1], in0=gqk_t[:, 0:1], scalar1=0.125,
                                                         scalar2=None, op0=ALU.mult), r=[R_gqk], w=[R_gqs])
        op("pool", lambda e: e.memset(gqs_t[64:128, 0:1], 0.0), r=[R_gqs], w=[R_gqs])
        op("pool", lambda e: e.memset(gqs_t[0:64, 1:2], 0.0), r=[R_gqs], w=[R_gqs])
        convp_v = convp_t.rearrange("p (c j) -> p c j", j=34)

        modT, R_modT = A.alloc("modT", 32 * 5)
        modT_v = modT.rearrange("p (i m) -> p i m", m=5)
        siluT, R_siluT = A.alloc("siluT", 40)
        siluT_v = siluT.rearrange("p (c m) -> p c m", m=5)
        gt1s = [A.alloc(f"gt1s{i}", D) for i in range(2)]
        expB, R_expB = A.alloc("expB", 4 * 1024, BF16)
        expB_v = expB.rearrange("p (k h q) -> p k h q", k=4, h=8)
        expBs, R_expBs = A.alloc("expBs", 512, BF16)
        expBs_v = expBs.rearrange("p (h q) -> p h q", h=8)
        const_end = A.ptr

        Win, R_Win = A.alloc("Win", 8 * 4608, BF16)
        Win_v = Win.rearrange("p (k n) -> p k n", k=8)
        Wco, R_Wco = A.alloc("Wco", 4 * 1024, BF16)
        Wco_v = Wco.rearrange("p (k n) -> p k n", k=4)
        Wao, R_Wao = A.alloc("Wao", 4 * 1024, BF16)
        Wao_v = Wao.rearrange("p (k n) -> p k n", k=4)
        Wo, R_Wo = A.alloc("Wo", 8 * 1024, BF16)
        Wo_v = Wo.rearrange("p (k n) -> p k n", k=8)
        w1_end = A.ptr

        cTt, R_cT = A.alloc("cTt", 40)
        dma(lambda e: e.dma_start(out=cTt.rearrange("p (c m) -> p c m", m=5),
                                  in_=cT.rearrange("(c p) m -> p c m", p=128)), w=[R_cT], dres=R_cT)
        op("act", lambda e: e.activation(out=siluT, in_=cTt, func=AF.Silu), r=[R_cT], w=[R_siluT])
        adast = [A.alloc(f"adast{i}", 8 * 512) for i in range(2)]
        bst = [A.alloc(f"bst{i}", 512) for i in range(2)]
        modblk = [A.alloc(f"modblk{i}", 512) for i in range(2)]
        wst = [A.alloc(f"wst{i}", 2304) for i in range(4)]
        relst, R_relst = A.alloc("relst", 1024)

        cast_rr = [0]
        cast_queues = [("pool",)]

        def load_cast_gen(dram_w, K, N, dst_v, R_dst, piece=2304):
            kch = K // 128
            for k in range(kch):
                for n0 in range(0, N, piece):
                    n1 = min(N, n0 + piece)
                    i = cast_rr[0]
                    cast_rr[0] += 1
                    stg, R_stg = wst[i % len(wst)]
                    qs = cast_queues[0]
                    dma(lambda e, stg=stg, k=k, n0=n0, n1=n1: e.dma_start(
                        out=stg[:, 0:n1 - n0], in_=dram_w[k * 128:(k + 1) * 128, n0:n1]),
                        w=[R_stg], dres=R_stg, q=qs[(i % len(wst)) % len(qs)])
                    if i % 2 == 1:
                        op("act", lambda e, stg=stg, k=k, n0=n0, n1=n1: e.activation(
                            out=dst_v[:, k, n0:n1], in_=stg[:, 0:n1 - n0], func=AF.Copy),
                           r=[R_stg], w=[R_dst])
                    else:
                        op("dve", lambda e, stg=stg, k=k, n0=n0, n1=n1: e.tensor_copy(
                            out=dst_v[:, k, n0:n1], in_=stg[:, 0:n1 - n0]), r=[R_stg], w=[R_dst])
                    yield

        def load_cast(*a_, **kw_):
            for _ in load_cast_gen(*a_, **kw_):
                pass

        def chain_gens(*gs):
            for g in gs:
                yield from g

        DBGS = int(os.environ.get("KDBG_SETUP", "9"))
        wgen = chain_gens(load_cast_gen(w_in, D, 4608, Win_v, R_Win),
                          load_cast_gen(w_conv_out, 512, D, Wco_v, R_Wco),
                          load_cast_gen(w_attn_out, 512, D, Wao_v, R_Wao),
                          load_cast_gen(w_o, D, D, Wo_v, R_Wo)) if DBGS >= 4 else iter(())

        kinds = {0: 0, 1: 0, 2: 1, 3: 1, 6: 2, 7: 2, 8: 3, 9: 3}
        for cb in range(12 if DBGS >= 2 else 0):
            sl = cb % 2
            ast, R_ast = adast[sl]
            bt, R_bt = bst[sl]
            mb, R_mb = modblk[sl]
            ast_v = ast.rearrange("p (k n) -> p k n", k=8)
            dma(lambda e, ast_v=ast_v, cb=cb: e.dma_start(
                out=ast_v, in_=w_ada[:, cb * 512:(cb + 1) * 512].rearrange("(k p) n -> p k n", p=128)),
                w=[R_ast], dres=R_ast)
            dma(lambda e, bt=bt, cb=cb: e.dma_start(
                out=bt[0:5, :], in_=b_ada[:, cb * 512:(cb + 1) * 512].partition_broadcast(5)),
                w=[R_bt], dres=R_bt)
            pb, R_pb = banks[cb % 2]
            for k in range(8):
                op("pe", lambda e, pb=pb, k=k, ast_v=ast_v: e.matmul(
                    pb[0:5, :], lhsT=siluT_v[:, k, :], rhs=ast_v[:, k, :], start=(k == 0), stop=(k == 7)),
                   r=[R_siluT, R_ast], w=[R_pb])
            op("dve", lambda e, pb=pb, mb=mb, bt=bt: e.tensor_tensor(
                out=mb[0:5, :], in0=pb[0:5, :], in1=bt[0:5, :], op=ALU.add), r=[R_pb, R_bt], w=[R_mb])
            dma(lambda e, mb=mb, cb=cb: e.dma_start(out=modrows[:, cb * 512:(cb + 1) * 512], in_=mb[0:5, :]),
                r=[R_mb], w=[R_modrows], dres=R_mb)
            for _ in range(3):
                next(wgen, None)
            if cb in kinds:
                kind = kinds[cb]
                pt, R_pt = banks[2 + cb % 2]
                for j in range(4):
                    op("pe", lambda e, pt=pt, mb=mb, j=j: e.transpose(
                        out=pt[:, j * 5:(j + 1) * 5], in_=mb[0:5, j * 128:(j + 1) * 128],
                        identity=identf[0:5, 0:5]), r=[R_mb, R_identf], w=[R_pt])
                base = kind * 8 + (cb % 2) * 4
                op("act", lambda e, pt=pt, base=base: e.activation(
                    out=modT[:, base * 5:(base + 4) * 5], in_=pt[:, 0:20], func=AF.Copy),
                   r=[R_pt], w=[R_modT])
        for kind, gcol in ((1, 0), (3, 8)):
            v = modT_v[:, kind * 8:(kind + 1) * 8, :]
            op("dve", lambda e, v=v: e.tensor_scalar(out=v, in0=v, scalar1=1.0, scalar2=None, op0=ALU.add),
               r=[R_modT], w=[R_modT])
            op("dve", lambda e, v=v, gcol=gcol: e.tensor_tensor(
                out=v, in0=v, in1=gfm_t[:, gcol:gcol + 8].unsqueeze(2).to_broadcast([128, 8, 5]), op=ALU.mult),
               r=[R_modT, R_gfm], w=[R_modT])

        def SH1(c, m): return modT_v[:, 0 + c, m:m + 1]
        def A1(c, m): return modT_v[:, 8 + c, m:m + 1]
        def SH2(c, m): return modT_v[:, 16 + c, m:m + 1]
        def A2(c, m): return modT_v[:, 24 + c, m:m + 1]

        def load_gt(off, tiles):
            for ap_, res_, parts in tiles:
                for (p0, p1, m) in parts:
                    dma(lambda e, ap_=ap_, p0=p0, p1=p1, m=m: e.dma_start(
                        out=ap_[p0:p1, :], in_=modrows[m:m + 1, off:off + D].partition_broadcast(p1 - p0)),
                        r=[R_modrows], w=[res_], dres=res_)

        if DBGS >= 3:
          load_gt(2 * D, [(gt1s[0][0], gt1s[0][1], [(0, 64, 1), (64, 128, 2)]),
                        (gt1s[1][0], gt1s[1][1], [(0, 64, 3), (64, 128, 4)])])

        for _ in wgen:
            pass
        if DBGS >= 4:
            gtmp, R_gtmp = relst, R_relst
            dma(lambda e: e.dma_start(out=gtmp, in_=modrows[0:1, 2 * D:3 * D].partition_broadcast(128)),
                r=[R_modrows], w=[R_gtmp], dres=R_gtmp)
            for k in range(8):
                op("dve", lambda e, k=k: e.tensor_tensor(out=Wo_v[:, k, :], in0=Wo_v[:, k, :], in1=gtmp, op=ALU.mult),
                   r=[R_Wo, R_gtmp], w=[R_Wo])

        for ti in range(4 if DBGS >= 5 else 0):
            dma(lambda e, ti=ti: e.dma_start(out=relst, in_=relbT[ti]), w=[R_relst], dres=R_relst)
            op("act", lambda e, ti=ti: e.activation(out=expB[:, ti * 1024:(ti + 1) * 1024], in_=relst,
                                                    func=AF.Copy), r=[R_relst], w=[R_expB])
        NEG = -30000.0
        op("pool", lambda e: e.memset(expB_v[0:64, 0, :, 64:128], NEG), r=[R_expB], w=[R_expB])
        op("pool", lambda e: e.memset(expB_v[64:128, 3, :, 0:64], NEG), r=[R_expB], w=[R_expB])
        op("pool", lambda e: e.tensor_copy(out=expBs_v, in_=expB_v[:, 3, :, 64:128]), r=[R_expB], w=[R_expBs])
        op("pool", lambda e: e.memset(expBs_v[0:64, :, :], NEG), r=[R_expBs], w=[R_expBs])
        TBL = [0, 1, 1, 2, 3]

        A.ptr = w1_end

        def alloc2(name, n, dt=F32):
            return [A.alloc(f"{name}{i}", n, dt) for i in range(2)]

        xA, R_xA = A.alloc("xA", D)
        xB, R_xB = A.alloc("xB", D)
        xn, R_xn = A.alloc("xn", D, BF16)
        hT2 = alloc2("hT", D, BF16)
        stat, R_stat = A.alloc("stat", 40)
        statn8, R_statn = A.alloc("statn", 8)
        statl, R_statl = A.alloc("statl", 8)
        sgl, R_sgl = A.alloc("sgl", 512)
        ufp, R_ufp = A.alloc("ufp", 512)
        ubf, R_ubf = A.alloc("ubf", 512, BF16)
        uT2raw = alloc2("uT", 4 * 2 * 94, BF16)
        uT2 = [(a_[:, 0:4 * 158], r_) for (a_, r_) in uT2raw]
        scrA, R_scrA = A.alloc("scrA", 512)
        qkn, R_qkn = A.alloc("qkn", D, BF16)
        qT2 = alloc2("qT", 1024, BF16)
        kring = [A.alloc(f"kr{i}", 512, BF16) for i in range(6)]
        vring = [A.alloc(f"vr{i}", 8 * 65, BF16) for i in range(6)]
        sg2 = alloc2("sg", 2 * D, BF16)
        ysqb, R_ysq = A.alloc("ysq", 512)
        acc, R_acc = A.alloc("acc", 512)
        R_accs = [R_acc] + [Res(f"acc{i}") for i in range(1, 4)]
        scrB, R_scrB = A.alloc("scrB", D)
        cact2 = alloc2("cact", 512, BF16)
        rowb, R_dg = A.alloc("dg", 256)
        PT, R_PT = A.alloc("PT", 5 * 512, BF16)
        onorm, R_onorm = A.alloc("onorm", 512, BF16)
        rden, R_rden = A.alloc("rden", 8)
        oT2 = alloc2("oT", 512, BF16)
        m1, R_m1 = A.alloc("m1", D)
        merged, R_merged = A.alloc("merged", D, BF16)
        mT, R_mT = A.alloc("mT", D, BF16)
        cst, R_cst = A.alloc("cst", 512)
        cbf, R_cbf = A.alloc("cbf", 512, BF16)
        uTs2 = uT2raw
        hst, R_hst = cst, R_cst
        hbf, R_hbf = cbf, R_cbf
        p1_end = A.ptr

        PTR = banks[0]
        PJ = [banks[1], banks[2], banks[3]]
        pj_rr = [0]

        ctx = {"st": "A"}
        PJB = [banks[5], banks[6], banks[7]]
        SC = [banks[5], banks[6]]
        PO = [banks[7], banks[4]]
        pj_rrb = [0]

        def pjbank():
            if ctx["st"] == "A":
                b = PJ[pj_rr[0] % 3]
                pj_rr[0] += 1
            else:
                b = PJB[pj_rrb[0] % 3]
                pj_rrb[0] += 1
            return b

        def PR():
            return PTR[1] if ctx["st"] == "A" else banks[4][1]

        def ptr_bf(n):
            t_ = PTR[0] if ctx["st"] == "A" else banks[4][0]
            return t_.bitcast(BF16)[:, 0:n]

        R_statn2 = [R_statn, Res("statn1")]

        def norm_load(blk):
            dma(lambda e: e.dma_start(out=xA, in_=blk["x"]), w=[R_xA], dres=R_xA)

        def norm_stats(blk):
            p_ = blk["t"] % 2
            statn = statn8[:, p_ * 4:p_ * 4 + 4]
            R_st = R_statn2[p_]
            op("act", multi(lambda e: e.activation(out=xn, in_=xA, func=AF.Square, accum_out=statn[:, 0:1])),
               r=[R_xA], w=[R_xn, R_st])
            op("pool", lambda e: e.tensor_scalar(out=statn[:, 1:2], in0=statn[:, 0:1], scalar1=1.0 / D, scalar2=EPS,
                                                 op0=ALU.mult, op1=ALU.add), r=[R_st], w=[R_st])
            op("pool", lambda e: e.tensor_tensor(out=statn[:, 2:3], in0=statn[:, 1:2], in1=MHALF, op=ALU.pow),
               r=[R_st, R_onesc], w=[R_st])

        def norm_to_hT(xbuf, R_x, hT, R_hT, mods, Afn, SHfn, p_):
            statn = statn8[:, p_ * 4:p_ * 4 + 4]
            op("act", lambda e: e.activation(out=xn, in_=xbuf, func=AF.Identity, scale=statn[:, 2:3]),
               r=[R_x, R_statn2[p_]], w=[R_xn])
            pt = ptr_bf(1024)
            for c in range(8):
                op("pe", lambda e, c=c: e.transpose(out=pt[:, c * 128:(c + 1) * 128],
                                                    in_=xn[:, c * 128:(c + 1) * 128], identity=ident),
                   r=[R_xn, R_ident], w=[PR()])
            hT_v = hT.rearrange("p (c t) -> p c t", c=8)
            ncol = 128 // len(mods)
            for c in range(8):
                for i, m in enumerate(mods):
                    op("act", lambda e, c=c, i=i, m=m: e.activation(
                        out=hT_v[:, c, i * ncol:(i + 1) * ncol],
                        in_=pt[:, c * 128 + i * ncol:c * 128 + (i + 1) * ncol],
                        func=AF.Identity, scale=Afn(c, m), bias=SHfn(c, m)),
                       r=[PR(), R_modT], w=[R_hT])
            return hT_v

        def proj(hT_v, R_hT, W_v, R_W, n0, n1, kch, bank):
            pb, R_pb = bank
            for k in range(kch):
                op("pe", lambda e, k=k: e.matmul(pb[:, 0:n1 - n0], lhsT=hT_v[:, k, :], rhs=W_v[:, k, n0:n1],
                                                 start=(k == 0), stop=(k == kch - 1)),
                   r=[R_hT, R_W], w=[R_pb])
            return pb, R_pb

        def transpose_to(src, R_src, nchunks, dst_v, R_dst, nrows=128, evac="act", scale=None, r_extra=()):
            pt = ptr_bf(nchunks * 128)
            for c in range(nchunks):
                op("pe", lambda e, c=c: e.transpose(out=pt[:, c * nrows:(c + 1) * nrows],
                                                    in_=src[0:nrows, c * 128:(c + 1) * 128],
                                                    identity=ident[0:nrows, 0:nrows]),
                   r=[R_src, R_ident], w=[PR()])
            pv = pt[:, 0:nchunks * nrows].rearrange("p (c t) -> p c t", c=nchunks)
            if scale is not None:
                op("act", lambda e: e.activation(out=dst_v, in_=pv, func=AF.Identity, scale=scale),
                   r=[PR()] + list(r_extra), w=[R_dst])
            elif evac == "act":
                op("act", lambda e: e.activation(out=dst_v, in_=pv, func=AF.Copy), r=[PR()], w=[R_dst])
            else:
                op("dve", lambda e: e.tensor_copy(out=dst_v, in_=pv), r=[PR()], w=[R_dst])

        def stageA(blk):
            t = blk["t"]
            par = t % 2
            kind = blk["kind"]
            hT, R_hT = hT2[par]
            hT_v = norm_to_hT(xA, R_xA, hT, R_hT, blk["mods"], A1, SH1, par)
            blk["hT_v"], blk["R_hT"] = hT_v, R_hT
            if blk.get("next") is not None:
                norm_load(blk["next"])
            yield
            need_u = kind != "halo" or blk["last_halo"]
            if need_u:
                ga_, R_ga = proj(hT_v, R_hT, Win_v, R_Win, 0, 512, 8, pjbank())
                gb_, R_gb = proj(hT_v, R_hT, Win_v, R_Win, 512, 1024, 8, pjbank())
                op("act", lambda e: e.activation(out=sgl, in_=gb_, func=AF.Sigmoid), r=[R_gb], w=[R_sgl])
                if blk["conv_out"]:
                    op("dve", lambda e: e.tensor_tensor(out=ufp, in0=ga_, in1=sgl, op=ALU.mult),
                       r=[R_ga, R_sgl], w=[R_ufp])
                    op("act", lambda e: e.activation(out=ubf, in_=ufp, func=AF.Copy), r=[R_ufp], w=[R_ubf])
                else:
                    op("dve", lambda e: e.tensor_tensor(out=ubf, in0=ga_, in1=sgl, op=ALU.mult),
                       r=[R_ga, R_sgl], w=[R_ubf])
                yield
                for (dst, r0, r1) in blk["conv_out"]:
                    dma(lambda e, dst=dst, r0=r0, r1=r1: e.dma_start(out=dst, in_=ufp[r0:r1, :]),
                        r=[R_ufp], w=[R_out], dres=R_ufp)
                if kind != "sample":
                    uT, R_uT = uT2[par]
                    uT_v = uT.rearrange("p (c t) -> p c t", c=4)
                    transpose_to(ubf, R_ubf, 4, uT_v[:, :, 30:158], R_uT,
                                 scale=(FLAG if kind == "halo" else ONE), r_extra=[R_onesc])
                    if kind != "halo":
                        uTp, R_uTp = uT2[1 - par]
                        uTp_v = uTp.rearrange("p (c t) -> p c t", c=4)
                        op("pool", lambda e: e.tensor_copy(out=uT_v[:, :, 0:30], in_=uTp_v[:, :, 128:158]),
                           r=[R_uTp], w=[R_uT])
                    blk["uT_v"], blk["R_uT"] = uT_v, R_uT
                else:
                    uTs, R_uTs = uTs2[par]
                    blk["R_uTs"] = R_uTs
                    uTs_v = uTs.rearrange("p (c i t) -> p c i t", c=4, i=2)
                    pt = ptr_bf(512)
                    for c in range(4):
                        op("pe", lambda e, c=c: e.transpose(out=pt[:, c * 128:(c + 1) * 128],
                                                            in_=ubf[:, c * 128:(c + 1) * 128], identity=ident),
                           r=[R_ubf, R_ident], w=[PR()])
                    for i in range(2):
                        op("act", lambda e, i=i: e.activation(
                            out=uTs_v[:, :, i, 30:94],
                            in_=pt.rearrange("p (c t) -> p c t", c=4)[:, :, i * 64:(i + 1) * 64], func=AF.Copy),
                           r=[PR()], w=[R_uTs])
                    for i in range(2):
                        seq = blk["seqs"][i]
                        dma(lambda e, seq=seq: e.dma_start(out=hst[0:30, :], in_=cconv[seq]), w=[R_hst], dres=R_hst)
                        op("pool", lambda e: e.tensor_copy(out=hbf[0:30, :], in_=hst[0:30, :]), r=[R_hst], w=[R_hbf])
                        pt2 = ptr_bf(120)
                        for c in range(4):
                            op("pe", lambda e, c=c: e.transpose(out=pt2[:, c * 30:(c + 1) * 30],
                                                                in_=hbf[0:30, c * 128:(c + 1) * 128],
                                                                identity=ident[0:30, 0:30]),
                               r=[R_hbf, R_ident], w=[PR()])
                        op("act", lambda e, i=i: e.activation(
                            out=uTs_v[:, :, i, 0:30], in_=pt2.rearrange("p (c t) -> p c t", c=4), func=AF.Copy),
                           r=[PR()], w=[R_uTs])
                    blk["uTs_v"] = uTs_v
            blk["u_done"] = True
            if blk.get("next") is not None:
                norm_stats(blk["next"])
            yield
            kb_, R_kb = proj(hT_v, R_hT, Win_v, R_Win, 1536, 2048, 8, pjbank())
            if kind != "halo":
                qb_, R_qb = proj(hT_v, R_hT, Win_v, R_Win, 1024, 1536, 8, pjbank())
            sq = scrA
            nqk = 2 if kind != "halo" else 1
            op("act", lambda e: e.activation(out=sq[:, 0:512], in_=kb_, func=AF.Square), r=[R_kb], w=[R_scrA])
            if kind != "halo":
                op("act", lambda e: e.activation(out=ufp, in_=qb_, func=AF.Square), r=[R_qb], w=[R_ufp])
            yield
            nh = 8 * nqk
            op("dve", lambda e: e.tensor_reduce(out=stat[:, 8:16],
                                                in_=sq[:, 0:512].rearrange("p (h d) -> p h d", d=64),
                                                axis=AX.X, op=ALU.add), r=[R_scrA], w=[R_stat])
            if kind != "halo":
                op("dve", lambda e: e.tensor_reduce(out=stat[:, 16:24],
                                                    in_=ufp.rearrange("p (h d) -> p h d", d=64),
                                                    axis=AX.X, op=ALU.add), r=[R_ufp], w=[R_stat])
            op("pool", lambda e: e.tensor_scalar(out=stat[:, 8:8 + nh], in0=stat[:, 8:8 + nh], scalar1=1.0 / 64,
                                                 scalar2=EPS, op0=ALU.mult, op1=ALU.add), r=[R_stat], w=[R_stat])
            op("pool", lambda e: e.tensor_tensor(out=stat[:, 24:24 + nh], in0=stat[:, 8:8 + nh],
                                                 in1=MHALF.to_broadcast([128, nh]), op=ALU.pow),
               r=[R_stat, R_onesc], w=[R_stat])
            op("dve", lambda e: e.tensor_tensor(
                out=qkn[:, 0:512].rearrange("p (h d) -> p h d", d=64),
                in0=kb_.rearrange("p (h d) -> p h d", d=64),
                in1=stat[:, 24:32].unsqueeze(2).to_broadcast([128, 8, 64]), op=ALU.mult),
               r=[R_kb, R_stat], w=[R_qkn])
            KV = int(os.environ.get("KDBG_KV", "9"))
            if blk["kout"] is not None and KV >= 1:
                kout = scrA[:, 0:512]
                op("dve", lambda e: e.tensor_tensor(
                    out=kout.rearrange("p (h d) -> p h d", d=64),
                    in0=kb_.rearrange("p (h d) -> p h d", d=64),
                    in1=stat[:, 24:32].unsqueeze(2).to_broadcast([128, 8, 64]), op=ALU.mult),
                   r=[R_kb, R_stat], w=[R_scrA])
                op("pool", lambda e: e.tensor_tensor(out=kout, in0=kout, in1=gkbc, op=ALU.mult),
                   r=[R_scrA, R_gkbc], w=[R_scrA])
                if KV >= 2:
                    dma(lambda e: e.dma_start(out=blk["kout"], in_=kout), r=[R_scrA], w=[R_out], dres=R_scrA)
            if kind != "halo":
                op("dve", lambda e: e.tensor_tensor(
                    out=qkn[:, 512:1024].rearrange("p (h d) -> p h d", d=64),
                    in0=qb_.rearrange("p (h d) -> p h d", d=64),
                    in1=stat[:, 32:40].unsqueeze(2).to_broadcast([128, 8, 64]), op=ALU.mult),
                   r=[R_qb, R_stat], w=[R_qkn])
            yield
            kT, R_kT = blk["kslot"]
            kT_v = kT.rearrange("p (c t) -> p c t", c=4)
            transpose_to(qkn[:, 0:512], R_qkn, 4, kT_v, R_kT, scale=gqk_t[:, 1:2], r_extra=[R_gqk])
            if kind != "halo":
                qT, R_qT = qT2[par]
                qT_v = qT.rearrange("p (o c t) -> p o c t", o=2, c=4)
                ptq = ptr_bf(512)
                for c in range(4):
                    op("pe", lambda e, c=c: e.transpose(out=ptq[:, c * 128:(c + 1) * 128],
                                                        in_=qkn[:, 512 + c * 128:512 + (c + 1) * 128], identity=ident),
                       r=[R_qkn, R_ident], w=[PR()])
                for o_ in range(2):
                    op("act", lambda e, o_=o_: e.activation(
                        out=qT_v[:, o_, :, :], in_=ptq.rearrange("p (c t) -> p c t", c=4), func=AF.Identity,
                        scale=gqs_t[:, o_:o_ + 1]), r=[PR(), R_gqs], w=[R_qT])
                blk["qT_v"], blk["R_qT"] = qT_v, R_qT
            yield
            vb_, R_vb = proj(hT_v, R_hT, Win_v, R_Win, 2048, 2560, 8, pjbank())
            vx, R_vx = blk["vslot"]
            vx_v = vx.rearrange("p (h d) -> p h d", d=65)
            fl = FLAG if kind == "halo" else ONE
            op("pool", lambda e: e.tensor_copy(out=vx_v[:, :, 64:65], in_=fl.unsqueeze(2).to_broadcast([128, 8, 1])),
               r=[R_onesc], w=[R_vx])
            op("act", lambda e: e.activation(out=vx_v[:, :, 0:64], in_=vb_.rearrange("p (h d) -> p h d", d=64),
                                             func=AF.Identity, scale=fl), r=[R_vb, R_onesc], w=[R_vx])
            if blk["vout"] is not None and KV >= 3:
                vout = sgl
                if KV != 5:
                    op("act", lambda e: e.activation(out=vout, in_=vb_, func=AF.Identity), r=[R_vb], w=[R_sgl])
                if KV != 4:
                    dma(lambda e: e.dma_start(out=blk["vout"], in_=vout), r=[R_sgl], w=[R_out], dres=R_sgl)
            yield

        def conv_ln(blk):
            acc_v = acc.rearrange("p (c t) -> p c t", c=4)
            sample = blk["kind"] == "sample"

            def src(c, j):
                if sample:
                    return blk["uTs_v"][:, c, :, j:j + 64]
                return blk["uT_v"][:, c, j:j + 128]

            def dst(c):
                if sample:
                    return acc_v[:, c, :].rearrange("p (i t) -> p i t", i=2)
                return acc_v[:, c, :]
            rU = [blk["R_uTs"]] if sample else [blk["R_uT"]]
            for j in range(31):
                if j % CONV_TAPS_PER_SEG == 0 and j > 0:
                    yield
                for c in range(4):
                    if j == 0:
                        op("dve", lambda e, c=c, j=j: e.tensor_scalar(
                            out=dst(c), in0=src(c, j), scalar1=convp_v[:, c, 0:1], scalar2=convp_v[:, c, 31:32],
                            op0=ALU.mult, op1=ALU.add), r=rU + [R_convp], w=[R_accs[c]])
                    else:
                        op("dve", lambda e, c=c, j=j: e.scalar_tensor_tensor(
                            out=dst(c), in0=src(c, j), scalar=convp_v[:, c, j:j + 1], in1=dst(c),
                            op0=ALU.mult, op1=ALU.add), r=rU + [R_convp, R_accs[c]], w=[R_accs[c]])
            return None

        def ln_part(blk):
            par = blk["t"] % 2
            acc_v = acc.rearrange("p (c t) -> p c t", c=4)
            ysq = ysqb
            op("act", lambda e: e.activation(out=ysq, in_=acc, func=AF.Square), r=R_accs, w=[R_ysq])
            pc, R_pc = pjbank()
            for c in range(4):
                op("pe", lambda e, c=c: e.matmul(pc[:, 0:1], lhsT=acc_v[:, c, :], rhs=ONE,
                                                 start=(c == 0), stop=(c == 3)), r=R_accs + [R_onesc], w=[R_pc])
            for c in range(4):
                op("pe", lambda e, c=c: e.matmul(pc[:, 1:2], lhsT=ysq[:, c * 128:(c + 1) * 128], rhs=ONE,
                                                 start=(c == 0), stop=(c == 3)), r=[R_ysq, R_onesc], w=[R_pc])
            lst = statl
            op("dve", lambda e: e.tensor_scalar(out=lst[:, 0:2], in0=pc[:, 0:2], scalar1=1.0 / 512, scalar2=None,
                                                op0=ALU.mult), r=[R_pc], w=[R_statl])
            yield
            op("dve", lambda e: e.tensor_tensor(out=lst[:, 2:3], in0=lst[:, 0:1], in1=lst[:, 0:1], op=ALU.mult),
               r=[R_statl], w=[R_statl])
            op("dve", lambda e: e.scalar_tensor_tensor(out=lst[:, 3:4], in0=lst[:, 1:2], scalar=EPS, in1=lst[:, 2:3],
                                                       op0=ALU.add, op1=ALU.subtract), r=[R_statl], w=[R_statl])
            op("pool", lambda e: e.tensor_tensor(out=lst[:, 4:5], in0=lst[:, 3:4], in1=MHALF, op=ALU.pow),
               r=[R_statl, R_onesc], w=[R_statl])
            op("dve", lambda e: e.scalar_tensor_tensor(out=lst[:, 5:6], in0=lst[:, 0:1], scalar=-1.0, in1=lst[:, 4:5],
                                                       op0=ALU.mult, op1=ALU.mult), r=[R_statl], w=[R_statl])
            dg = rowb[:, 0:256]
            for i in range(2):
                op("dve", lambda e, i=i: e.tensor_scalar(out=dg[:, i * 128:(i + 1) * 128], in0=identf,
                                                         scalar1=lst[:, 4 + i:5 + i], scalar2=None, op0=ALU.mult),
                   r=[R_statl, R_identf], w=[R_dg])
            yield
            pb2, R_pb2 = pjbank()
            op("pe", lambda e: e.matmul(pb2[:, 0:256], lhsT=onesrow, rhs=dg, start=True, stop=True),
               r=[R_dg, R_onesrow], w=[R_pb2])
            op("dve", lambda e: e.tensor_tensor(out=acc_v, in0=acc_v,
                                                in1=pb2[:, 0:128].unsqueeze(1).to_broadcast([128, 4, 128]),
                                                op=ALU.mult), r=R_accs + [R_pb2], w=R_accs)
            op("dve", lambda e: e.tensor_tensor(out=acc_v, in0=acc_v,
                                                in1=pb2[:, 128:256].unsqueeze(1).to_broadcast([128, 4, 128]),
                                                op=ALU.add), r=R_accs + [R_pb2], w=R_accs)
            yield
            cact, R_cact = cact2[par]
            cact_v = cact.rearrange("p (c t) -> p c t", c=4)
            blk["cact_v"], blk["R_cact"] = cact_v, R_cact
            for c in range(4):
                op("act", lambda e, c=c: e.activation(out=cact_v[:, c, :], in_=acc_v[:, c, :], func=AF.Silu,
                                                      scale=convp_v[:, c, 32:33], bias=convp_v[:, c, 33:34]),
                   r=R_accs + [R_convp], w=[R_cact])
            blk["ln_done"] = True
            yield

        def attend(qT_v, R_qT, q0, nq, kblocks, vblocks, tables, onorm_rows):
            PT_v = PT.rearrange("p (k h q) -> p k h q", k=5, h=4)
            DBGA = int(os.environ.get("KDBG_ATT", "9"))
            for hg in range(2):
                for kb in range(5):
                    kT_v, R_k = kblocks[kb]
                    sb_, R_sb = SC[(hg * 5 + kb) % 2]
                    tb, R_tb = tables[kb]
                    for hh in range(4):
                        h = hg * 4 + hh
                        hp, od = h // 2, h % 2
                        op("pe", lambda e, hh=hh, hp=hp, od=od, kT_v=kT_v, sb_=sb_: e.matmul(
                            sb_[:, hh * nq:(hh + 1) * nq], lhsT=kT_v[:, hp, :],
                            rhs=qT_v[:, od, hp, q0:q0 + nq], start=True, stop=False),
                           r=[R_k, R_qT], w=[R_sb])
                        op("pe", lambda e, hh=hh, h=h, tb=tb, sb_=sb_: e.matmul(
                            sb_[:, hh * nq:(hh + 1) * nq], lhsT=ident, rhs=tb[:, h, :], start=False, stop=True),
                           r=[R_tb, R_ident], w=[R_sb])
                    pv = PT_v[:, kb, :, 0:nq]
                    op("act", lambda e, sb_=sb_, pv=pv: e.activation(
                        out=pv, in_=sb_[:, 0:4 * nq].rearrange("p (h q) -> p h q", h=4), func=AF.Exp),
                       r=[R_sb], w=[R_PT])
                    if kb % 2 == 1:
                        yield
                ob, R_ob = PO[hg]
                if DBGA < 3:
                    continue
                for hh in range(4):
                    h = hg * 4 + hh
                    for kb in range(5):
                        vx_v, R_v = vblocks[kb]
                        op("pe", lambda e, hh=hh, h=h, kb=kb, vx_v=vx_v, ob=ob: e.matmul(
                            ob[0:nq, hh * 65:(hh + 1) * 65], lhsT=PT_v[:, kb, hh, 0:nq], rhs=vx_v[:, h, :],
                            start=(kb == 0), stop=(kb == 4)), r=[R_PT, R_v], w=[R_ob])
                ob_v = ob[0:nq, 0:260].rearrange("p (h d) -> p h d", d=65)
                if DBGA < 4:
                    continue
                op("dve", lambda e, ob_v=ob_v, hg=hg: e.reciprocal(
                    out=rden[0:nq, hg * 4:(hg + 1) * 4].unsqueeze(2), in_=ob_v[:, :, 64:65]),
                   r=[R_ob], w=[R_rden])
                op("dve", lambda e, ob_v=ob_v, hg=hg: e.tensor_tensor(
                    out=onorm_rows[:, hg * 256:(hg + 1) * 256].rearrange("p (h d) -> p h d", d=64),
                    in0=ob_v[:, :, 0:64],
                    in1=rden[0:nq, hg * 4:(hg + 1) * 4].unsqueeze(2).to_broadcast([nq, 4, 64]), op=ALU.mult),
                   r=[R_ob, R_rden], w=[R_onorm])
                yield

        def stageS2(blk):
            t = blk["t"]
            par = t % 2
            kind = blk["kind"]
            hT_v, R_hT = blk["hT_v"], blk["R_hT"]
            sg, R_sg = sg2[par]
            oT, R_oT = oT2[par]
            blk["sg"], blk["oT"] = (sg, R_sg), (oT, R_oT)
            for i in range(4):
                gb_, R_gb = proj(hT_v, R_hT, Win_v, R_Win, 2560 + i * 512, 3072 + i * 512, 8, pjbank())
                op("act", lambda e, i=i, gb_=gb_: e.activation(out=sg[:, i * 512:(i + 1) * 512], in_=gb_,
                                                               func=AF.Sigmoid), r=[R_gb], w=[R_sg])
                yield
            DBGB = int(os.environ.get("KDBG_B", "9"))
            if DBGB < 2:
                return
            oT_v = oT.rearrange("p (c t) -> p c t", c=4)
            tabs = [(expB_v[:, TBL[kb], :, :], R_expB) for kb in range(5)]
            if kind == "main":
                kbl = [(kring[s][0].rearrange("p (c t) -> p c t", c=4), kring[s][1]) for s in blk["kslots"]]
                vbl = [(vring[s][0].rearrange("p (h d) -> p h d", d=65), vring[s][1]) for s in blk["kslots"]]
                yield from attend(blk["qT_v"], blk["R_qT"], 0, 128, kbl, vbl, tabs, onorm)
                transpose_to(onorm, R_onorm, 4, oT_v, R_oT)
                yield
            else:
                for i in range(2):
                    seq = blk["seqs"][i]
                    for kb in range(4):
                        dma(lambda e, seq=seq, kb=kb: e.dma_start(out=cst, in_=ck[seq, kb * 128:(kb + 1) * 128, :]),
                            w=[R_cst], dres=R_cst)
                        op("pool", lambda e: e.tensor_copy(out=cbf, in_=cst), r=[R_cst], w=[R_cbf])
                        kT, R_kT = kring[2 + kb]
                        transpose_to(cbf, R_cbf, 4, kT.rearrange("p (c t) -> p c t", c=4), R_kT)
                        dma(lambda e, seq=seq, kb=kb: e.dma_start(out=cst, in_=cv[seq, kb * 128:(kb + 1) * 128, :]),
                            w=[R_cst], dres=R_cst)
                        vx, R_vx = vring[2 + kb]
                        vx_v = vx.rearrange("p (h d) -> p h d", d=65)
                        op("dve", lambda e, vx_v=vx_v: e.tensor_copy(
                            out=vx_v[:, :, 0:64], in_=cst.rearrange("p (h d) -> p h d", d=64)), r=[R_cst], w=[R_vx])
                        op("pool", lambda e, vx_v=vx_v: e.tensor_copy(
                            out=vx_v[:, :, 64:65], in_=ONE.unsqueeze(2).to_broadcast([128, 8, 1])),
                           r=[R_onesc], w=[R_vx])
                        yield
                    sl = blk["kslots"][-1]
                    kbl = [(kring[s][0].rearrange("p (c t) -> p c t", c=4), kring[s][1]) for s in (2, 3, 4, 5, sl)]
                    vbl = [(vring[s][0].rearrange("p (h d) -> p h d", d=65), vring[s][1]) for s in (2, 3, 4, 5, sl)]
                    tb = list(tabs)
                    tb[4] = (expB_v[:, 3, :, 0:64], R_expB) if i == 0 else (expBs_v, R_expBs)
                    tb = [(tt[:, :, 0:64], rr) for (tt, rr) in tb[:4]] + [tb[4]]
                    yield from attend(blk["qT_v"], blk["R_qT"], i * 64, 64, kbl, vbl, tb, onorm[0:64, :])
                    pt = ptr_bf(512)
                    for c in range(4):
                        op("pe", lambda e, c=c, pt=pt: e.transpose(
                            out=pt[:, c * 64:(c + 1) * 64], in_=onorm[0:64, c * 128:(c + 1) * 128],
                            identity=ident[0:64, 0:64]), r=[R_onorm, R_ident], w=[PR()])
                    op("act", lambda e, i=i, pt=pt: e.activation(
                        out=oT_v[:, :, i * 64:(i + 1) * 64],
                        in_=pt[:, 0:256].rearrange("p (c t) -> p c t", c=4), func=AF.Copy),
                       r=[PR()], w=[R_oT])
                    yield

        def stageS3(blk):
            t = blk["t"]
            par = t % 2
            kind = blk["kind"]
            sg, R_sg = blk["sg"]
            oT, R_oT = blk["oT"]
            oT_v = oT.rearrange("p (c t) -> p c t", c=4)
            R_cact = blk["R_cact"]
            if kind == "sample" and not wo_reloaded[0]:
                wo_reloaded[0] = True
                for k in range(8):
                    dma(lambda e, k=k: e.dma_start(out=xA, in_=w_o[k * 128:(k + 1) * 128, :]), w=[R_xA], dres=R_xA)
                    if k % 2 == 0:
                        op("dve", lambda e, k=k: e.tensor_copy(out=Wo_v[:, k, :], in_=xA), r=[R_xA], w=[R_Wo])
                    else:
                        op("act", lambda e, k=k: e.activation(out=Wo_v[:, k, :], in_=xA, func=AF.Copy),
                           r=[R_xA], w=[R_Wo])
                yield
            dma(lambda e: e.dma_start(out=xB, in_=blk["x"]), w=[R_xB], dres=R_xB)
            cact_v = blk["cact_v"]
            for half in range(2):
                cb_, R_cb = proj(cact_v, R_cact, Wco_v, R_Wco, half * 512, (half + 1) * 512, 4, pjbank())
                op("dve", lambda e, half=half, cb_=cb_: e.tensor_tensor(
                    out=m1[:, half * 512:(half + 1) * 512], in0=cb_, in1=sg[:, half * 512:(half + 1) * 512],
                    op=ALU.mult), r=[R_cb, R_sg], w=[R_m1])
                if half == 1:
                    yield
            for half in range(2):
                ab_, R_ab = proj(oT_v, R_oT, Wao_v, R_Wao, half * 512, (half + 1) * 512, 4, pjbank())
                tmp = scrB[:, half * 512:(half + 1) * 512]
                op("dve", lambda e, half=half, ab_=ab_, tmp=tmp: e.tensor_tensor(
                    out=tmp, in0=ab_, in1=sg[:, 1024 + half * 512:1024 + (half + 1) * 512], op=ALU.mult),
                   r=[R_ab, R_sg], w=[R_scrB])
                op("dve", lambda e, half=half, tmp=tmp: e.tensor_tensor(
                    out=merged[:, half * 512:(half + 1) * 512], in0=tmp, in1=m1[:, half * 512:(half + 1) * 512],
                    op=ALU.add), r=[R_scrB, R_m1], w=[R_merged])
                yield
            mT_v = mT.rearrange("p (c t) -> p c t", c=8)
            transpose_to(merged, R_merged, 8, mT_v, R_mT)
            yield
            for half in range(2):
                ob_, R_ob = proj(mT_v, R_mT, Wo_v, R_Wo, half * 512, (half + 1) * 512, 8, pjbank())
                x1h = scrB[:, half * 512:(half + 1) * 512]
                if kind == "sample":
                    gt, R_gt = blk["gt1"]
                    op("dve", lambda e, half=half, ob_=ob_, x1h=x1h, gt=gt: e.tensor_tensor(
                        out=x1h, in0=ob_, in1=gt[:, half * 512:(half + 1) * 512], op=ALU.mult),
                       r=[R_ob, R_gt], w=[R_scrB])
                    op("dve", lambda e, half=half, x1h=x1h: e.tensor_tensor(
                        out=x1h, in0=x1h, in1=xB[:, half * 512:(half + 1) * 512], op=ALU.add),
                       r=[R_scrB, R_xB], w=[R_scrB])
                else:
                    op("dve", lambda e, half=half, ob_=ob_, x1h=x1h: e.tensor_tensor(
                        out=x1h, in0=ob_, in1=xB[:, half * 512:(half + 1) * 512], op=ALU.add),
                       r=[R_ob, R_xB], w=[R_scrB])
                yield
            xi = blk["x1idx"]
            dma(lambda e: e.dma_start(out=x1s[xi * 128:(xi + 1) * 128, :], in_=scrB), r=[R_scrB], w=[R_x1s[xi]],
                dres=R_scrB)

        blocks = []
        for bi in range(NHB + NPB):
            halo = bi < NHB
            b = {"t": bi, "kind": "halo" if halo else "main", "last_halo": bi == NHB - 1,
                 "x": xp[bi * 128:(bi + 1) * 128, :], "mods": [0],
                 "kslot": kring[bi % 6], "vslot": vring[bi % 6],
                 "kslots": [(bi - 4 + i) % 6 for i in range(5)],
                 "kout": None, "vout": None, "conv_out": [], "gt1": None, "x1idx": bi - NHB}
            if bi >= NHB + NPB - 4:
                r0 = (bi - (NHB + NPB - 4)) * 128
                b["kout"] = k_p[r0:r0 + 128, :]
                b["vout"] = v_p[r0:r0 + 128, :]
            if bi == NHB + NPB - 1:
                b["conv_out"] = [(conv_p[:, :], 98, 128)]
            blocks.append(b)
        for sbi in range(2):
            bi = NHB + NPB + sbi
            sl = bi % 6
            b = {"t": bi, "kind": "sample", "last_halo": False, "x": xs[sbi * 128:(sbi + 1) * 128, :],
                 "mods": [1 + 2 * sbi, 2 + 2 * sbi], "kslot": kring[sl], "vslot": vring[sl], "kslots": [sl],
                 "seqs": [2 * sbi, 2 * sbi + 1],
                 "kout": k_s[sbi * 128:(sbi + 1) * 128, :], "vout": v_s[sbi * 128:(sbi + 1) * 128, :],
                 "conv_out": [(conv_s[2 * sbi], 34, 64), (conv_s[2 * sbi + 1], 98, 128)],
                 "gt1": gt1s[sbi], "x1idx": NPB + sbi}
            blocks.append(b)

        nb = len(blocks)
        for i_ in range(nb - 1):
            blocks[i_]["next"] = blocks[i_ + 1]
        norm_load(blocks[0])
        norm_stats(blocks[0])
        DBG_STEPS = int(os.environ.get("KDBG_STEPS", "-1"))
        DBG_P2 = int(os.environ.get("KDBG_P2", "-1"))
        if DBG_STEPS >= 0:
            nb = DBG_STEPS - 1
        wo_reloaded = [False]

        def gS2(blk):
            yield from ln_part(blk)
            yield from stageS2(blk)

        for step in range(nb + 2):
            chains = []
            if step < nb:
                chains.append(["A", stageA(blocks[step]), "A"])
            b2 = blocks[step - 1] if 1 <= step <= nb and blocks[step - 1]["kind"] != "halo" else None
            b3 = blocks[step - 2] if 2 <= step <= nb + 1 and blocks[step - 2]["kind"] != "halo" else None
            if os.environ.get("KDBG_NOB"):
                b2 = b3 = None
            if b2 is not None:
                chains.append(["B", gS2(b2), "S2"])
            if b3 is not None:
                chains.append(["B", stageS3(b3), "S3"])
            chains.sort(key=lambda it: CHAIN_ORDER.index(it[2]))
            want_conv = step < nb and blocks[step]["kind"] != "halo"
            conv_added = False
            while chains or (want_conv and not conv_added):
                for item in list(chains):
                    if item not in chains:
                        continue
                    ctx["st"] = item[0]
                    try:
                        next(item[1])
                    except StopIteration:
                        chains.remove(item)
                    if CONV_BOOST and item[2] != "C":
                        for cch_ in [c_ for c_ in chains if c_[2] == "C"]:
                            ctx["st"] = cch_[0]
                            try:
                                next(cch_[1])
                            except StopIteration:
                                chains.remove(cch_)
                a_done = blocks[step].get("u_done") if step < nb else True
                ln_ok = b2 is None or b2.get("ln_done")
                if want_conv and not conv_added and a_done and ln_ok:
                    chains.append(["A", conv_ln(blocks[step]), "C"])
                    conv_added = True
        ctx["st"] = "A"

        A.ptr = const_end
        Wf1, R_Wf1 = A.alloc("Wf1", 8 * 2 * FFN, BF16)
        Wf1_v = Wf1.rearrange("p (k n) -> p k n", k=8)
        Wf2, R_Wf2 = A.alloc("Wf2", 22 * D, BF16)
        Wf2_v = Wf2.rearrange("p (k n) -> p k n", k=22)
        gt2p, R_gt2p = A.alloc("gt2p", D)
        gt2s = [A.alloc(f"gt2s{i}", D) for i in range(2)]
        wst2 = [A.alloc(f"wstb{i}", 2304) for i in range(3)]
        wst[:] = wst2
        cast_queues[0] = ("sp", "pool")
        load_cast(w_ffn_in, D, 2 * FFN, Wf1_v, R_Wf1, piece=2304)
        load_cast(w_ffn_out, FFN, D, Wf2_v, R_Wf2, piece=1024)
        load_gt(5 * D, [(gt2p, R_gt2p, [(0, 128, 0)]),
                        (gt2s[0][0], gt2s[0][1], [(0, 64, 1), (64, 128, 2)]),
                        (gt2s[1][0], gt2s[1][1], [(0, 64, 3), (64, 128, 4)])])
        A.ptr -= 3 * 2304
        x1b = alloc2("x1b", D)
        xn2, R_xn2 = A.alloc("xn2", D, BF16)
        h2 = alloc2("h2T", D, BF16)
        stat2, R_stat2 = A.alloc("stat2", 8)
        sil = alloc2("sil", 512)
        actb, R_actb = A.alloc("actb", FFN, BF16)
        aT = alloc2("aT", 22 * 128, BF16)
        ALLB = banks[1:8]
        rr2 = [0]

        def bank2():
            b = ALLB[rr2[0] % 7]
            rr2[0] += 1
            return b

        widths = [(0, 512), (512, 1024), (1024, 1536), (1536, 2048), (2048, 2560), (2560, 2816)]
        fblocks = [(i, y_p[i * 128:(i + 1) * 128, :], [0], (gt2p, R_gt2p)) for i in range(NPB)]
        fblocks += [(NPB + i, y_s[i * 128:(i + 1) * 128, :], [1 + 2 * i, 2 + 2 * i], gt2s[i]) for i in range(2)]

        def ffnA(fb):
            idx, ydst, mods, gt = fb
            par = idx % 2
            xb, R_xb = x1b[par]
            dma(lambda e: e.dma_start(out=xb, in_=x1s[idx * 128:(idx + 1) * 128, :]), r=[R_x1s[idx]], w=[R_xb],
                dres=R_xb)
            hT, R_hT = h2[par]
            op("act", multi(lambda e: e.activation(out=xn2, in_=xb, func=AF.Square, accum_out=stat2[:, 0:1])),
               r=[R_xb], w=[R_xn2, R_stat2])
            op("pool", lambda e: e.tensor_scalar(out=stat2[:, 1:2], in0=stat2[:, 0:1], scalar1=1.0 / D, scalar2=EPS,
                                                 op0=ALU.mult, op1=ALU.add), r=[R_stat2], w=[R_stat2])
            op("pool", lambda e: e.tensor_tensor(out=stat2[:, 2:3], in0=stat2[:, 1:2], in1=MHALF, op=ALU.pow),
               r=[R_stat2, R_onesc], w=[R_stat2])
            op("act", lambda e: e.activation(out=xn2, in_=xb, func=AF.Identity, scale=stat2[:, 2:3]),
               r=[R_xb, R_stat2], w=[R_xn2])
            pt = ptr_bf(1024)
            for c in range(8):
                op("pe", lambda e, c=c: e.transpose(out=pt[:, c * 128:(c + 1) * 128],
                                                    in_=xn2[:, c * 128:(c + 1) * 128], identity=ident),
                   r=[R_xn2, R_ident], w=[PR()])
            hT_v = hT.rearrange("p (c t) -> p c t", c=8)
            ncol = 128 // len(mods)
            for c in range(8):
                for i, m in enumerate(mods):
                    op("act", lambda e, c=c, i=i, m=m: e.activation(
                        out=hT_v[:, c, i * ncol:(i + 1) * ncol],
                        in_=pt[:, c * 128 + i * ncol:c * 128 + (i + 1) * ncol],
                        func=AF.Identity, scale=A2(c, m), bias=SH2(c, m)), r=[PR(), R_modT], w=[R_hT])
            return hT_v, R_hT

        R_actw = [Res(f"actw{i}") for i in range(6)]
        aT_res = [[Res(f"aT{p}_{i}") for i in range(6)] for p in range(2)]
        accb = [banks[1], banks[2]]
        PAIRB = [banks[3], banks[4], banks[5], banks[6], banks[7]]
        pair_rr = [0]

        def pbank():
            b_ = PAIRB[pair_rr[0] % 5]
            pair_rr[0] += 1
            return b_

        def ffnB(fb, hT_v, R_hT, next_fb):
            idx, ydst, mods, gt = fb
            par = idx % 2
            xb, R_xb = x1b[par]
            aTt, R_aT0 = aT[par]
            aT_v = aTt.rearrange("p (c t) -> p c t", c=22)
            RaT = aT_res[par]
            nxt = None

            def tr(wi):
                n0, n1 = widths[wi]
                c0, nch = n0 // 128, (n1 - n0) // 128
                pt = ptr_bf(nch * 128)
                for c in range(nch):
                    op("pe", lambda e, c=c, c0=c0, pt=pt: e.transpose(
                        out=pt[:, c * 128:(c + 1) * 128], in_=actb[:, (c0 + c) * 128:(c0 + c + 1) * 128],
                        identity=ident), r=[R_actw[wi], R_ident], w=[PR()])
                pv = pt.rearrange("p (c t) -> p c t", c=nch)
                wr = [RaT[wi]] + ([R_aT0] if wi == 0 else [])
                if wi % 2 == 0:
                    op("act", lambda e, pv=pv, c0=c0, nch=nch: e.activation(out=aT_v[:, c0:c0 + nch, :], in_=pv,
                                                                            func=AF.Copy), r=[PR()], w=wr)
                else:
                    op("dve", lambda e, pv=pv, c0=c0, nch=nch: e.tensor_copy(out=aT_v[:, c0:c0 + nch, :], in_=pv),
                       r=[PR()], w=wr)

            def mm2(wi):
                n0, n1 = widths[wi]
                for c in range(n0 // 128, n1 // 128):
                    for half in range(2):
                        ab_, R_ab = accb[half]
                        op("pe", lambda e, c=c, half=half, ab_=ab_: e.matmul(
                            ab_[:, 0:512], lhsT=aT_v[:, c, :], rhs=Wf2_v[:, c, half * 512:(half + 1) * 512],
                            start=(c == 0), stop=(c == 21)), r=[RaT[wi], R_Wf2], w=[R_ab])

            for wi, (n0, n1) in enumerate(widths):
                gb_, R_gb = proj(hT_v, R_hT, Wf1_v, R_Wf1, n0, n1, 8, pbank())
                ub_, R_ub = proj(hT_v, R_hT, Wf1_v, R_Wf1, FFN + n0, FFN + n1, 8, pbank())
                sl_, R_sl = sil[wi % 2]
                op("act", lambda e, gb_=gb_, sl_=sl_, n0=n0, n1=n1: e.activation(
                    out=sl_[:, 0:n1 - n0], in_=gb_[:, 0:n1 - n0], func=AF.Silu), r=[R_gb], w=[R_sl])
                op("dve", lambda e, ub_=ub_, sl_=sl_, n0=n0, n1=n1: e.tensor_tensor(
                    out=actb[:, n0:n1], in0=ub_[:, 0:n1 - n0], in1=sl_[:, 0:n1 - n0], op=ALU.mult),
                   r=[R_ub, R_sl], w=[R_actw[wi]])
                if wi >= 1:
                    tr(wi - 1)
                if wi >= 2:
                    mm2(wi - 2)
                if wi == 2 and next_fb is not None:
                    nxt = (next_fb,) + ffnA(next_fb)
            tr(5)
            mm2(4)
            mm2(5)
            gtt, R_gt = gt
            for half in range(2):
                ob_, R_ob = accb[half]
                tmp_, R_tmp = sil[half]
                yh = xb[:, half * 512:(half + 1) * 512]
                op("dve", lambda e, half=half, ob_=ob_, tmp_=tmp_: e.tensor_tensor(
                    out=tmp_, in0=ob_, in1=gtt[:, half * 512:(half + 1) * 512], op=ALU.mult),
                   r=[R_ob, R_gt], w=[R_tmp])
                op("dve", lambda e, half=half, yh=yh, tmp_=tmp_: e.tensor_tensor(
                    out=yh, in0=yh, in1=tmp_, op=ALU.add),
                   r=[R_tmp, R_xb], w=[R_xb])
            dma(lambda e: e.dma_start(out=ydst, in_=xb), r=[R_xb], w=[R_out], dres=R_xb)
            return nxt

        if DBG_P2 >= 0:
            fblocks = fblocks[:DBG_P2]
        if DBG_STEPS >= 0 and DBG_P2 < 0:
            fblocks = []
        cur = None
        if fblocks:
            cur = (fblocks[0],) + ffnA(fblocks[0])
        for i_, fb in enumerate(fblocks):
            nfb = fblocks[i_ + 1] if i_ + 1 < len(fblocks) else None
            cur = ffnB(cur[0], cur[1], cur[2], nfb)

        S.emit()
    return nc


def _prep_inputs(inp):
    f = np.float32
    x_prompt = np.asarray(inp["x_prompt"], f)
    x_sample = np.asarray(inp["x_sample"], f)
    c_prompt = np.asarray(inp["c_prompt"], f)
    c_sample = np.asarray(inp["c_sample"], f)
    cache_conv = np.asarray(inp["cache_conv"], f)[0]
    cache_k = np.asarray(inp["cache_k"], f)[0].reshape(32, 512, 512)
    cache_v = np.asarray(inp["cache_v"], f)[0].reshape(32, 512, 512)
    rel_bias = np.asarray(inp["rel_bias"], f)[0]
    key = np.arange(128)[:, None]
    q = np.arange(128)[None, :]
    tabs = []
    for kb in (0, 1, 3, 4):
        ridx = np.clip(512 + q - 128 * kb - key, -128, 128) + 128
        tabs.append(np.transpose(rel_bias[:, ridx], (1, 0, 2)).reshape(128, 8 * 128))
    relbT = np.ascontiguousarray(np.stack(tabs, 0))
    w_dw = np.asarray(inp["w_dw"], f)[0]
    convp = np.concatenate([w_dw.T, np.asarray(inp["b_dw"], f)[0][:, None],
                            np.asarray(inp["conv_ln_g"], f)[0][:, None],
                            np.asarray(inp["conv_ln_b"], f)[0][:, None]], axis=1)
    convp = np.ascontiguousarray(convp.reshape(4, 128, 34).transpose(1, 0, 2).reshape(128, 4 * 34))
    g1 = np.asarray(inp["norm1_g"], f)[0].reshape(8, 128).T
    g2 = np.asarray(inp["norm2_g"], f)[0].reshape(8, 128).T
    gfm = np.ascontiguousarray(np.concatenate([g1, g2], axis=1))
    gq = np.asarray(inp["q_norm_g"], f)[0]
    gk = np.asarray(inp["k_norm_g"], f)[0]
    gqk = np.ascontiguousarray(np.stack([np.tile(gq, 2), np.tile(gk, 2)], axis=1))
    gkrow = np.ascontiguousarray(np.tile(gk, 8)[None, :])
    shared = {
        "relbT": relbT, "w_ada": np.asarray(inp["w_ada"], f)[0], "b_ada": np.asarray(inp["b_ada"], f),
        "gfm": gfm, "w_in": np.asarray(inp["w_in"], f)[0], "convp": convp,
        "w_conv_out": np.asarray(inp["w_conv_out"], f)[0], "gqk": gqk, "gkrow": gkrow,
        "w_attn_out": np.asarray(inp["w_attn_out"], f)[0], "w_o": np.asarray(inp["w_o"], f)[0],
        "w_ffn_in": np.asarray(inp["w_ffn_in"], f)[0], "w_ffn_out": np.asarray(inp["w_ffn_out"], f)[0],
    }
    in_maps = []
    for core in range(8):
        b, seg = core // 4, core % 4
        start = seg * 4096
        xpc = np.zeros((NHB * 128 + NPB * 128, D), f)
        if seg > 0:
            xpc[:] = x_prompt[b, start - 512:start + 4096]
        else:
            xpc[512:] = x_prompt[b, 0:4096]
        m = dict(shared)
        m["xp"] = xpc
        m["xs"] = np.ascontiguousarray(x_sample[4 * core:4 * core + 4].reshape(256, D))
        m["cT"] = np.ascontiguousarray(
            np.stack([c_prompt[b]] + [c_sample[4 * core + i] for i in range(4)], axis=1))
        m["flag"] = np.full((128, 1), 1.0 if seg > 0 else 0.0, f)
        m["cconv"] = np.ascontiguousarray(cache_conv[4 * core:4 * core + 4])
        m["ck"] = np.ascontiguousarray(cache_k[4 * core:4 * core + 4])
        m["cv"] = np.ascontiguousarray(cache_v[4 * core:4 * core + 4])
        in_maps.append(m)
    return in_maps


_NC_CACHE = {}


def kernel(**inp):
    if "nc" not in _NC_CACHE:
        _NC_CACHE["nc"] = build()
    nc = _NC_CACHE["nc"]
    in_maps = _prep_inputs(inp)
    res = run_bass_kernel_spmd(nc, in_maps, core_ids=list(range(8)))
    r = res.results
    f = np.float32
    y_prompt = np.stack([np.concatenate([r[b * 4 + s]["y_p"] for s in range(4)], 0) for b in range(2)], 0)
    y_sample = np.concatenate([r[c]["y_s"].reshape(4, 64, D) for c in range(8)], 0)
    conv_p = np.stack([r[3]["conv_p"], r[7]["conv_p"]], 0)[None]
    k_p = np.stack([r[3]["k_p"], r[7]["k_p"]], 0).reshape(1, 2, 512, 8, 64)
    v_p = np.stack([r[3]["v_p"], r[7]["v_p"]], 0).reshape(1, 2, 512, 8, 64)
    conv_s = np.concatenate([r[c]["conv_s"] for c in range(8)], 0)[None]
    k_s = np.concatenate([r[c]["k_s"].reshape(4, 64, 8, 64) for c in range(8)], 0)[None]
    v_s = np.concatenate([r[c]["v_s"].reshape(4, 64, 8, 64) for c in range(8)], 0)[None]
    outs = (y_prompt, y_sample, conv_p, k_p, v_p, conv_s, k_s, v_s)
    return tuple(np.ascontiguousarray(o, dtype=f) for o in outs)
```
